# Optimizing a Trainium2 kernel written in Bass

```python
import jax, jax.numpy as jnp
from jax import lax
import numpy as np

D_MODEL = 1024
BATCH = 8
SEQ = 2048
DEPTH = 2
DEC_BATCH = 32
DEC_SEQ = 4
PAST_LEN = 8192
PAGE_SIZE = 128

NSA_HEADS = 8
NSA_KV_HEADS = 2
NSA_HEAD_DIM = 64
CMP_STRIDE = 16
CMP_BLOCK = 32
CMP_RATIO = CMP_BLOCK // CMP_STRIDE
CMP_HIDDEN = 64
SEL_BLOCK = 64
N_SEL = 16
WINDOW = 512
NSA_QBLK = 64
RET_HEADS = 4
RET_DK = 128
RET_DV = 256
RET_CHUNK = 128
ROPE_BASE = 10000.0
D_FF = 2816
CONV_W = 3
RMS_EPS = 1e-6
GN_EPS = 1e-5
NEG = -1e30
FORCE_BONUS = 1e4

A_Q = NSA_HEADS * NSA_HEAD_DIM
A_KV = 3 * 2 * NSA_KV_HEADS * NSA_HEAD_DIM
A_G = 3 * NSA_HEADS
B_QK = RET_HEADS * RET_DK
B_V = RET_HEADS * RET_DV
IN_SPLITS = (A_Q, A_KV, A_G, B_QK, B_QK, B_V, B_V, D_MODEL, D_MODEL)
N_IN = A_Q + A_KV + A_G + 2 * B_QK + 2 * B_V + 2 * D_MODEL

kernel_name = 'nsa_retention_convffn_hybrid_step'


def split_points():
    pts, acc = [], 0
    for s in IN_SPLITS[:-1]:
        acc += s
        pts.append(acc)
    return pts


def rmsnorm(x, w):
    xf = x.astype(jnp.float32)
    y = xf * lax.rsqrt(jnp.mean(xf * xf, axis=-1, keepdims=True) + RMS_EPS)
    return (y * w.astype(jnp.float32)).astype(x.dtype)


def rotary(x, pos):
    half = x.shape[-1] // 2
    inv = jnp.exp(-jnp.log(ROPE_BASE) * jnp.arange(half, dtype=jnp.float32) / half)
    ang = pos.astype(jnp.float32)[:, None] * inv[None, :]
    cos = jnp.cos(ang)[None, :, None, :]
    sin = jnp.sin(ang)[None, :, None, :]
    xf = x.astype(jnp.float32)
    x1, x2 = xf[..., :half], xf[..., half:]
    return jnp.concatenate([x1 * cos - x2 * sin, x1 * sin + x2 * cos], axis=-1).astype(x.dtype)


def gather_pages(pool, page_table):
    g = pool[page_table]
    return g.reshape(g.shape[0], g.shape[1] * g.shape[2], *g.shape[3:])


def keep_last(rows, n):
    t = rows.shape[1]
    if t >= n:
        return rows[:, t - n:]
    return jnp.pad(rows, ((0, 0), (n - t, 0)) + ((0, 0),) * (rows.ndim - 2))


def compress_kv(kv, cmp_pos, cmp_w1, cmp_w2):
    B, L, _, G, dh = kv.shape
    n_chunk = L // CMP_STRIDE
    n_cmp = n_chunk - CMP_RATIO + 1
    chunks = kv[:, :n_chunk * CMP_STRIDE].reshape(B, n_chunk, CMP_STRIDE, 2, G, dh)
    w1 = cmp_w1.reshape(2, CMP_RATIO, CMP_STRIDE, dh, CMP_HIDDEN)
    bias = jnp.einsum('eld,eldh->eh', cmp_pos, cmp_w1.reshape(2, CMP_BLOCK, dh, CMP_HIDDEN))
    hid = bias[None, None, :, None, :]
    for m in range(CMP_RATIO):
        hid = hid + jnp.einsum('bnsegd,esdh->bnegh', chunks[:, m:m + n_cmp], w1[:, m])
    return jnp.einsum('bnegh,ehd->bnegd', jax.nn.gelu(hid), cmp_w2)


def sel_overlap(n_cmp, n_sel):
    cs = jnp.arange(n_cmp) * CMP_STRIDE
    ss = jnp.arange(n_sel) * SEL_BLOCK
    ok = (cs[:, None] < ss[None, :] + SEL_BLOCK) & (cs[:, None] + CMP_BLOCK > ss[None, :])
    return ok.astype(jnp.float32)


def nsa_block(q, qpos, gates, kc, vc, cmp_end, overlap, ks_blk, vs_blk, kw, vw, kwpos):
    f32 = jnp.float32
    B, Tb, H, dh = q.shape
    G = kc.shape[2]
    scale = dh ** -0.5
    qg = q.reshape(B, Tb, G, H // G, dh)
    s_c = jnp.einsum('btgqd,bngd->bgqtn', qg, kc).astype(f32) * scale
    c_ok = cmp_end[None, :] <= qpos[:, None]
    p_c = jnp.where(c_ok, jax.nn.softmax(jnp.where(c_ok, s_c, NEG), axis=-1), 0.0)
    o_c = jnp.einsum('bgqtn,bngd->btgqd', p_c.astype(vc.dtype), vc)
    imp = jnp.einsum('bgqtn,nj->bgtj', p_c, overlap)
    n_sel = ks_blk.shape[2]
    j = jnp.arange(n_sel)[None, :]
    cur = (qpos // SEL_BLOCK)[:, None]
    s_ok = j * SEL_BLOCK <= qpos[:, None]
    forced = (j == 0) | (j == cur) | (j == cur - 1)
    score = jnp.where(s_ok, imp + jnp.where(forced, FORCE_BONUS, 0.0), NEG)
    top_s, idx = lax.top_k(score, min(N_SEL, n_sel))
    take = jax.vmap(jax.vmap(lambda blk, ix: blk[ix]))
    k_s = take(ks_blk, idx)
    v_s = take(vs_blk, idx)
    tok = idx[..., None] * SEL_BLOCK + jnp.arange(SEL_BLOCK)
    t_ok = (top_s > 0.5 * NEG)[..., None] & (tok <= qpos[None, None, :, None, None])
    s_s = jnp.einsum('btgqd,bgtksd->bgqtks', qg, k_s).astype(f32) * scale
    s_s = jnp.where(t_ok[:, :, None], s_s, NEG)
    shp = s_s.shape
    p_s = jax.nn.softmax(s_s.reshape(*shp[:4], -1), axis=-1).reshape(shp)
    o_s = jnp.einsum('bgqtks,bgtksd->btgqd', p_s.astype(v_s.dtype), v_s)
    s_w = jnp.einsum('btgqd,bsgd->bgqts', qg, kw).astype(f32) * scale
    rel = qpos[:, None] - kwpos[None, :]
    w_ok = (rel >= 0) & (rel < WINDOW) & (kwpos[None, :] >= 0)
    p_w = jax.nn.softmax(jnp.where(w_ok, s_w, NEG), axis=-1)
    o_w = jnp.einsum('bgqts,bsgd->btgqd', p_w.astype(vw.dtype), vw)
    g = jax.nn.sigmoid(gates.astype(f32)).reshape(B, Tb, G, H // G, 3, 1).astype(q.dtype)
    o = g[..., 0, :] * o_c + g[..., 1, :] * o_s + g[..., 2, :] * o_w
    return o.reshape(B, Tb, H * dh)


def nsa_attention(q, gates, kv_cmp, kv_sel, kv_win, q_pos0, w_pos0, cmp_pos, cmp_w1, cmp_w2):
    B, Tq, H, dh = q.shape
    L = kv_cmp.shape[1]
    G = kv_cmp.shape[3]
    comp = compress_kv(kv_cmp, cmp_pos, cmp_w1, cmp_w2)
    kc, vc = comp[:, :, 0], comp[:, :, 1]
    n_cmp = comp.shape[1]
    cmp_end = jnp.arange(n_cmp) * CMP_STRIDE + (CMP_BLOCK - 1)
    n_sel = -(-L // SEL_BLOCK)
    overlap = sel_overlap(n_cmp, n_sel)
    kvs = jnp.pad(kv_sel, ((0, 0), (0, n_sel * SEL_BLOCK - L), (0, 0), (0, 0), (0, 0)))
    kvs = kvs.reshape(B, n_sel, SEL_BLOCK, 2, G, dh).transpose(3, 0, 4, 1, 2, 5)
    past_w = kv_win.shape[1] - Tq
    pad_n = max(WINDOW - past_w, 0)
    kvw = jnp.pad(kv_win, ((0, 0), (pad_n, 0), (0, 0), (0, 0), (0, 0)))
    base = past_w + pad_n - WINDOW
    kw_pos0 = w_pos0 - pad_n
    qb_len = NSA_QBLK if Tq % NSA_QBLK == 0 else Tq
    nb = Tq // qb_len
    qb = q.reshape(B, nb, qb_len, H, dh).swapaxes(0, 1)
    gb = gates.reshape(B, nb, qb_len, H, 3).swapaxes(0, 1)

    def one(args):
        i, q_i, g_i = args
        offs = i * qb_len
        qpos = q_pos0 + offs + jnp.arange(qb_len)
        start = base + offs
        w_i = lax.dynamic_slice_in_dim(kvw, start, WINDOW + qb_len, axis=1)
        kwpos = kw_pos0 + start + jnp.arange(WINDOW + qb_len)
        return nsa_block(q_i, qpos, g_i, kc, vc, cmp_end, overlap, kvs[0], kvs[1],
                         w_i[:, :, 0], w_i[:, :, 1], kwpos)

    out = lax.map(one, (jnp.arange(nb), qb, gb))
    return out.swapaxes(0, 1).reshape(B, Tq, H * dh)


def retention(q, k, v, state0):
    f32 = jnp.float32
    B, T, H, _ = q.shape
    dv = v.shape[-1]
    C = RET_CHUNK if T % RET_CHUNK == 0 else T
    n = T // C
    log_g = jnp.log(1.0 - jnp.exp2(-5.0 - jnp.arange(H, dtype=f32)))
    i = jnp.arange(C, dtype=f32)
    diff = i[:, None] - i[None, :]
    dmask = jnp.where(diff >= 0, jnp.exp(jnp.maximum(diff, 0.0)[None] * log_g[:, None, None]), 0.0)
    q_dec = jnp.exp((i + 1.0)[None, :] * log_g[:, None])[..., None]
    k_dec = jnp.exp((C - 1.0 - i)[None, :] * log_g[:, None])[..., None]
    s_dec = jnp.exp(C * log_g)[:, None, None]

    def chunks(a):
        return a.astype(f32).reshape(B, n, C, H, a.shape[-1]).transpose(1, 0, 3, 2, 4)

    def step(S, qkv):
        qc, kc, vc = qkv
        inner = jnp.einsum('bhcd,bhed->bhce', qc, kc) * dmask
        o = jnp.einsum('bhce,bhev->bhcv', inner, vc) + jnp.einsum('bhcd,bhdv->bhcv', qc * q_dec, S)
        S = S * s_dec + jnp.einsum('bhcd,bhcv->bhdv', kc * k_dec, vc)
        return S, o

    S, o = lax.scan(step, state0.astype(f32), (chunks(q), chunks(k), chunks(v)))
    return o.transpose(1, 0, 3, 2, 4).reshape(B, T, H, dv), S


def group_norm(o, w):
    mu = jnp.mean(o, axis=-1, keepdims=True)
    var = jnp.mean(jnp.square(o - mu), axis=-1, keepdims=True)
    return (o - mu) * lax.rsqrt(var + GN_EPS) * w.astype(jnp.float32)


def conv_ffn(h, prev, w_in, conv_w, conv_b, w_out):
    T = h.shape[1]
    a, b = jnp.split(h @ w_in, 2, axis=-1)
    a_ext = jnp.concatenate([prev.astype(a.dtype), a], axis=1)
    u = conv_b
    for j in range(CONV_W):
        u = u + a_ext[:, j:j + T] * conv_w[j]
    y = (jax.nn.gelu(u) * b) @ w_out
    return y, a_ext[:, T:]


def decoder_layer(x, c, pos0, past_cmp, past_sel, win_buf, ret_state, conv_prev, win_keep,
                  norm1_w, ada_w, ada_b, w_in, cmp_pos, cmp_w1, cmp_w2, w_oa, ret_gn_w, w_ob, w_o,
                  norm2_w, ffn_w_in, ffn_conv_w, ffn_conv_b, ffn_w_out):
    B, T, _ = x.shape
    mod = jax.nn.silu(c) @ ada_w + ada_b
    sh1, sc1, g1, sh2, sc2, g2 = jnp.split(mod[:, None, :], 6, axis=-1)
    h = rmsnorm(x, norm1_w) * (1 + sc1) + sh1
    aq, akv, ag, rq, rk, rv, rg, ga, gb = jnp.split(h @ w_in, split_points(), axis=-1)
    akv = akv.reshape(B, T, 3, 2, NSA_KV_HEADS, NSA_HEAD_DIM)
    kv_cmp, kv_sel, kv_win = akv[:, :, 0], akv[:, :, 1], akv[:, :, 2]
    win_all = jnp.concatenate([win_buf.astype(x.dtype), kv_win], axis=1)
    y_a = nsa_attention(aq.reshape(B, T, NSA_HEADS, NSA_HEAD_DIM), ag.reshape(B, T, NSA_HEADS, 3),
                        jnp.concatenate([past_cmp.astype(x.dtype), kv_cmp], axis=1),
                        jnp.concatenate([past_sel.astype(x.dtype), kv_sel], axis=1),
                        win_all, pos0, pos0 - win_buf.shape[1], cmp_pos, cmp_w1, cmp_w2) @ w_oa
    pos = pos0 + jnp.arange(T)
    rq = rotary(rq.reshape(B, T, RET_HEADS, RET_DK), pos)
    rk = rotary(rk.reshape(B, T, RET_HEADS, RET_DK), pos) * (RET_DK ** -0.5)
    ro, ret_new = retention(rq, rk, rv.reshape(B, T, RET_HEADS, RET_DV), ret_state)
    ro = group_norm(ro, ret_gn_w.reshape(RET_HEADS, RET_DV)).reshape(B, T, B_V).astype(x.dtype)
    y_b = (jax.nn.silu(rg) * ro) @ w_ob
    x = x + g1 * ((jax.nn.sigmoid(ga) * y_a + jax.nn.sigmoid(gb) * y_b) @ w_o)
    h2 = rmsnorm(x, norm2_w) * (1 + sc2) + sh2
    f, conv_new = conv_ffn(h2, conv_prev, ffn_w_in, ffn_conv_w, ffn_conv_b, ffn_w_out)
    x = x + g2 * f
    return x, (kv_cmp, kv_sel, keep_last(win_all, win_keep), ret_new, conv_new)


def setup_inputs(seed: int = 0) -> dict:
    key = jax.random.key(seed)
    ks = jax.random.split(key, 32)
    f32 = jnp.float32
    n_pages = PAST_LEN // PAGE_SIZE
    used = DEC_BATCH * n_pages
    n_phys = used + max(1, used // 4)
    win_keep = min(WINDOW, PAST_LEN)
    kvs = (2, NSA_KV_HEADS, NSA_HEAD_DIM)

    def nrm(k, shape, s):
        return jax.random.normal(k, shape, f32) * s

    page_table = jax.random.permutation(ks[9], n_phys)[:used].reshape(DEC_BATCH, n_pages).astype(jnp.int32)
    return {
        'x_prompt': nrm(ks[0], (BATCH, SEQ, D_MODEL), 1.0),
        'x_sample': nrm(ks[1], (DEC_BATCH, DEC_SEQ, D_MODEL), 1.0),
        'c_prompt': nrm(ks[2], (BATCH, D_MODEL), 1.0),
        'c_sample': nrm(ks[3], (DEC_BATCH, D_MODEL), 1.0),
        'cache_cmp_kv': nrm(ks[4], (DEPTH, n_phys, PAGE_SIZE) + kvs, 1.0),
        'cache_sel_kv': nrm(ks[5], (DEPTH, n_phys, PAGE_SIZE) + kvs, 1.0),
        'state_win_kv': nrm(ks[6], (DEPTH, DEC_BATCH, win_keep) + kvs, 1.0),
        'state_ret': nrm(ks[7], (DEPTH, DEC_BATCH, RET_HEADS, RET_DK, RET_DV), 0.5),
        'state_conv': nrm(ks[8], (DEPTH, DEC_BATCH, CONV_W - 1, D_FF), 1.0),
        'page_table': page_table,
        'norm1_w': 1.0 + nrm(ks[10], (DEPTH, D_MODEL), 0.02),
        'ada_w': nrm(ks[11], (DEPTH, D_MODEL, 6 * D_MODEL), D_MODEL ** -0.5),
        'ada_b': nrm(ks[12], (DEPTH, 6 * D_MODEL), 0.01),
        'w_in': nrm(ks[13], (DEPTH, D_MODEL, N_IN), D_MODEL ** -0.5),
        'cmp_pos': nrm(ks[14], (DEPTH, 2, CMP_BLOCK, NSA_HEAD_DIM), 0.1),
        'cmp_w1': nrm(ks[15], (DEPTH, 2, CMP_BLOCK * NSA_HEAD_DIM, CMP_HIDDEN), (CMP_BLOCK * NSA_HEAD_DIM) ** -0.5),
        'cmp_w2': nrm(ks[16], (DEPTH, 2, CMP_HIDDEN, NSA_HEAD_DIM), CMP_HIDDEN ** -0.5),
        'w_oa': nrm(ks[17], (DEPTH, A_Q, D_MODEL), A_Q ** -0.5),
        'ret_gn_w': 1.0 + nrm(ks[18], (DEPTH, B_V), 0.02),
        'w_ob': nrm(ks[19], (DEPTH, B_V, D_MODEL), B_V ** -0.5),
        'w_o': nrm(ks[20], (DEPTH, D_MODEL, D_MODEL), D_MODEL ** -0.5),
        'norm2_w': 1.0 + nrm(ks[21], (DEPTH, D_MODEL), 0.02),
        'ffn_w_in': nrm(ks[22], (DEPTH, D_MODEL, 2 * D_FF), D_MODEL ** -0.5),
        'ffn_conv_w': nrm(ks[23], (DEPTH, CONV_W, D_FF), CONV_W ** -0.5),
        'ffn_conv_b': nrm(ks[24], (DEPTH, D_FF), 0.01),
        'ffn_w_out': nrm(ks[25], (DEPTH, D_FF, D_MODEL), D_FF ** -0.5),
        'normf_w': 1.0 + nrm(ks[26], (D_MODEL,), 0.02),
    }


def reference(x_prompt, x_sample, c_prompt, c_sample, cache_cmp_kv, cache_sel_kv, state_win_kv,
              state_ret, state_conv, page_table, norm1_w, ada_w, ada_b, w_in, cmp_pos, cmp_w1,
              cmp_w2, w_oa, ret_gn_w, w_ob, w_o, norm2_w, ffn_w_in, ffn_conv_w, ffn_conv_b,
              ffn_w_out, normf_w):
    B = x_prompt.shape[0]
    win_keep = state_win_kv.shape[2]
    past_len = page_table.shape[1] * cache_cmp_kv.shape[2]
    empty = jnp.zeros((B, 0, 2, NSA_KV_HEADS, NSA_HEAD_DIM), x_prompt.dtype)
    ret0 = jnp.zeros((B, RET_HEADS, RET_DK, RET_DV), jnp.float32)
    conv0 = jnp.zeros((B, CONV_W - 1, D_FF), x_prompt.dtype)
    xp, xs = x_prompt, x_sample
    outs_p, outs_s = [], []
    for l in range(DEPTH):
        w = (norm1_w[l], ada_w[l], ada_b[l], w_in[l], cmp_pos[l], cmp_w1[l], cmp_w2[l], w_oa[l],
             ret_gn_w[l], w_ob[l], w_o[l], norm2_w[l], ffn_w_in[l], ffn_conv_w[l], ffn_conv_b[l],
             ffn_w_out[l])
        xp, sp = decoder_layer(xp, c_prompt, 0, empty, empty, empty, ret0, conv0, win_keep, *w)
        xs, ss = decoder_layer(xs, c_sample, past_len,
                               gather_pages(cache_cmp_kv[l], page_table),
                               gather_pages(cache_sel_kv[l], page_table),
                               state_win_kv[l], state_ret[l], state_conv[l], win_keep, *w)
        outs_p.append(sp)
        outs_s.append(ss)
    y_prompt = rmsnorm(xp, normf_w)
    y_sample = rmsnorm(xs, normf_w)

    def stk(outs, i, dt):
        return jnp.stack([o[i] for o in outs]).astype(dt)

    return (y_prompt, y_sample,
            stk(outs_p, 0, cache_cmp_kv.dtype), stk(outs_s, 0, cache_cmp_kv.dtype),
            stk(outs_p, 1, cache_sel_kv.dtype), stk(outs_s, 1, cache_sel_kv.dtype),
            stk(outs_p, 2, state_win_kv.dtype), stk(outs_s, 2, state_win_kv.dtype),
            stk(outs_p, 3, state_ret.dtype), stk(outs_s, 3, state_ret.dtype),
            stk(outs_p, 4, state_conv.dtype), stk(outs_s, 4, state_conv.dtype))
```

```python
import contextlib
import numpy as np
import concourse.bass as bass
import concourse.mybir as mybir
from concourse.bass_utils import run_bass_kernel_spmd

F32 = mybir.dt.float32
BF16 = mybir.dt.bfloat16
I32 = mybir.dt.int32
AF = mybir.ActivationFunctionType
ALU = mybir.AluOpType
AX = mybir.AxisListType

D = 1024
T = 2048
NS = 16
NT = T + NS
L = 2
DFF = 2816
NIN = 6424
NEGB = -30000.0
COMPUTE = ("scalar", "vector", "gpsimd", "tensor")
ENGS = ("sync", "scalar", "vector", "gpsimd", "tensor")


class Res:
    __slots__ = ("name", "w", "rs", "rdma")

    def __init__(self, name):
        self.name = name
        self.w = None
        self.rs = {}
        self.rdma = []


class Op:
    __slots__ = ("eng", "fn", "deps", "inc", "cnt", "dma", "key", "pos", "bar")

    def __init__(self, eng, fn, dma, key):
        self.eng = eng
        self.fn = fn
        self.deps = []
        self.inc = False
        self.cnt = 0
        self.dma = dma
        self.key = key
        self.bar = None


class Sched:
    def __init__(self, nc):
        self.nc = nc
        self.ops = {e: [] for e in ENGS}
        self.last = {e: None for e in ENGS}
        self.dmakeys = {}

    def add(self, eng, fn, reads=(), writes=(), dma=False, key=None, nochain=False):
        op = Op(eng, fn, dma, key)
        deps = {}

        def dep(o):
            if o is None:
                return
            if (not o.dma) and o.eng == "tensor" and eng == "tensor" and not dma:
                return
            k = ("d", id(o)) if o.dma else o.eng
            p = deps.get(k)
            if p is None or p.pos < o.pos:
                deps[k] = o

        for r in reads:
            dep(r.w)
        for r in writes:
            if not (nochain and r.w is not None and r.w.dma and r.w.key == key):
                dep(r.w)
            for o in r.rs.values():
                dep(o)
            for o in r.rdma:
                dep(o)
        op.pos = len(self.ops[eng])
        if dma:
            self.dmakeys[key] = self.dmakeys.get(key, 0) + 16
            op.cnt = self.dmakeys[key]
        for r in reads:
            if dma:
                r.rdma.append(op)
            else:
                r.rs[eng] = op
        for r in writes:
            r.w = op
            r.rs = {}
            r.rdma = []
        op.deps = list(deps.values())
        for o in op.deps:
            if not o.dma:
                o.inc = True
        self.ops[eng].append(op)
        if not dma:
            self.last[eng] = op
        return op

    def barrier(self):
        lasts = [o for o in self.last.values() if o is not None]
        keys_final = dict(self.dmakeys)
        for e in ENGS:
            op = Op(e, None, False, None)
            op.pos = len(self.ops[e])
            op.deps = list(lasts)
            for o in lasts:
                o.inc = True
            op.bar = keys_final
            self.ops[e].append(op)

    def emit(self):
        nc = self.nc
        mx = {}
        for e in ENGS:
            c = 0
            for op in self.ops[e]:
                if op.dma or op.bar is not None:
                    continue
                if op.inc:
                    c += 1
                    op.cnt = c
            mx[e] = (c, len(self.ops[e]))
        print("sched: ops/incs per engine", mx, "dma keys", len(self.dmakeys))
        with contextlib.ExitStack() as st:
            sems = {}
            for e in ENGS:
                sems[e] = st.enter_context(nc.semaphore("s_" + e))
            for i, k in enumerate(self.dmakeys):
                sems[("k", k)] = st.enter_context(nc.semaphore("d%d" % i))
            block = st.enter_context(nc.Block())
            final_totals = dict(self.dmakeys)

            def run(ename, eobj, is_final):
                waited = {}

                def w(sk, cnt):
                    if waited.get(sk, 0) >= cnt:
                        return
                    waited[sk] = cnt
                    eobj.wait_ge(sems[sk], cnt)

                for op in self.ops[ename]:
                    for o in op.deps:
                        if o.dma:
                            w(("k", o.key), o.cnt)
                        else:
                            w(o.eng, o.cnt)
                    if op.bar is not None:
                        for k, tot in op.bar.items():
                            w(("k", k), tot)
                        continue
                    ins = op.fn(eobj)
                    if op.dma:
                        ins.then_inc(sems[("k", op.key)], 16)
                    elif op.inc:
                        ins.then_inc(sems[ename], 1)
                if is_final:
                    for k, tot in final_totals.items():
                        w(("k", k), tot)

            @block.sync
            def _(e):
                run("sync", e, True)

            @block.scalar
            def _(e):
                run("scalar", e, False)

            @block.vector
            def _(e):
                run("vector", e, False)

            @block.gpsimd
            def _(e):
                run("gpsimd", e, False)

            @block.tensor
            def _(e):
                run("tensor", e, False)


def _consts():
    f32 = np.float32
    c = {}
    c["ident"] = np.eye(128, dtype=f32)
    n = np.arange(128)[:, None]
    cm = np.zeros((4, 128, 512), f32)
    for ch in range(4):
        t = ch * 512 + np.arange(512)[None, :]
        cm[ch] = np.where((16 * n + 31 <= t) & (n < 127), 0.0, NEGB)
    c["cmask"] = cm
    wm = np.zeros((8, 128, 512), f32)
    sl = np.arange(128)[:, None]
    col = np.arange(512)[None, :]
    qt = col // 128
    tl = col % 128
    for i in range(-4, 4):
        d = qt - i
        ok = ((d == 0) & (sl <= tl)) | ((d >= 1) & (d <= 3)) | ((d == 4) & (tl < sl))
        wm[i + 4] = np.where(ok, 0.0, NEGB)
    c["wmask"] = wm
    e = np.zeros((128, 2048), f32)
    for j in range(32):
        e[j, 64 * j:64 * j + 64] = 1.0
    c["eall"] = e
    rc = np.zeros((128, 33), f32)
    nn = np.arange(128)[:, None]
    jj = np.arange(32)[None, :]
    rc[:, :32] = ((16 * nn < 64 * jj + 64) & (16 * nn + 32 > 64 * jj) & (nn < 127)).astype(f32)
    rc[:, 32] = 1.0
    c["rcc"] = rc
    t = (np.arange(16)[None, :, None] * 128 + np.arange(128)[:, None, None])
    j = np.arange(32)[None, None, :]
    sok = (64 * j <= t)
    cur = t // 64
    forced = (j == 0) | (j == cur) | (j == cur - 1)
    c["sok"] = sok.astype(f32)
    c["addc"] = np.where(sok, np.where(forced, 1e4, 0.0), -1e30).astype(f32)
    half = 64
    inv = np.exp(-np.log(f32(10000.0)) * np.arange(half, dtype=f32) / f32(half)).astype(f32)
    pos = np.concatenate([np.arange(T), 8192 + (np.arange(NS) % 4)]).astype(f32)
    ang = (pos[:, None] * inv[None, :]).astype(f32)
    cs, sn = np.cos(ang).astype(f32), np.sin(ang).astype(f32)
    c["cosF"] = np.concatenate([cs.T, cs.T], 0).copy()
    c["sinF"] = np.concatenate([-sn.T, sn.T], 0).copy()
    ct = np.zeros((128, 17, 64), f32)
    stt = np.zeros((128, 17, 64), f32)
    for tt in range(16):
        ct[:, tt] = cs[tt * 128:(tt + 1) * 128]
        stt[:, tt] = sn[tt * 128:(tt + 1) * 128]
    ct[:16, 16] = cs[T:]
    stt[:16, 16] = sn[T:]
    c["cosT"] = ct
    c["sinT"] = stt
    lg = np.log(f32(1.0) - np.exp2(f32(-5.0) - np.arange(4, dtype=f32))).astype(f32)
    i = np.arange(128, dtype=f32)
    diff = i[:, None] - i[None, :]
    dm = np.where(diff >= 0, np.exp(np.maximum(diff, 0.0)[None] * lg[:, None, None]), 0.0).astype(f32)
    ginv = np.exp(-(i + 1.0)[None, :] * lg[:, None]).astype(f32)
    dm2 = np.where(diff[None] >= 0, ginv[:, None, :], 0.0).astype(f32)
    c["dmT"] = dm2.transpose(2, 0, 1).copy()
    qd = np.exp((i + 1.0)[None, :] * lg[:, None]).astype(f32)
    c["qdec"] = np.broadcast_to(qd[None], (128, 4, 128)).copy()
    kd = np.exp((128 - 1.0 - i)[None, :] * lg[:, None]).astype(f32)
    c["kdec"] = kd.T.copy()
    sdec = np.exp(f32(128) * lg).astype(f32)
    i4 = np.arange(16)
    same = (i4[:, None] // 4) == (i4[None, :] // 4)
    d4 = (i4[:, None] % 4 - i4[None, :] % 4).astype(f32)
    dms = np.where(same[None] & (d4[None] >= 0), np.exp(np.maximum(d4, 0.0)[None] * lg[:, None, None]), 0.0).astype(f32)
    dmTs = np.zeros((128, 4, 16), f32)
    ginvs = np.exp(-((i4 % 4) + 1.0).astype(f32)[None, :] * lg[:, None]).astype(f32)
    dms2 = np.where(same[None] & (d4[None] >= 0), ginvs[:, None, :], 0.0).astype(f32)
    dmTs[:16] = dms2.transpose(2, 0, 1)
    c["dmTs"] = dmTs
    qds = np.exp(((i4 % 4) + 1.0).astype(f32)[None, :] * lg[:, None]).astype(f32)
    c["qdecs"] = np.broadcast_to(qds[None], (128, 4, 16)).copy()
    kds = np.zeros((128, 4), f32)
    kds[:16] = np.exp((4 - 1.0 - (i4 % 4)).astype(f32)[None, :] * lg[:, None]).astype(f32).T
    c["kdecs"] = kds
    sdecs = np.exp(f32(4) * lg).astype(f32)
    sm = np.zeros((128, 4), f32)
    for s in range(4):
        sm[4 * s:4 * s + 4, s] = 1.0
    c["seqmask"] = sm
    nn = np.arange(512)[:, None]
    jj = np.arange(129)[None, :]
    ovs = np.zeros((512, 130), f32)
    ovs[:, :129] = ((16 * nn < 64 * jj + 64) & (16 * nn + 32 > 64 * jj) & (nn < 511)).astype(f32)
    ovs[:511, 129] = 1.0
    c["ovs"] = ovs
    ad = np.zeros((4, 129), f32)
    ad[:, [0, 127, 128]] = 1e4
    c["addcs"] = ad
    r16 = np.arange(16)
    wb = np.where(np.arange(512)[None, :] >= (r16 % 4)[:, None] + 1, 0.0, NEGB).astype(f32)
    c["winb"] = wb
    nm = np.zeros((4, 16, 16), f32)
    for s_ in range(4):
        key = np.arange(16)[None, :]
        nm[s_] = np.where((key // 4 == s_) & (key % 4 <= (r16 % 4)[:, None]), 0.0, NEGB)
    c["newm"] = nm.transpose(1, 0, 2).copy()
    selm = np.zeros((16, 4), f32)
    selm[r16, r16 % 4] = 1.0
    c["selm"] = selm
    c["selmT"] = selm.T.copy()
    return c, [float(v) for v in sdec], [float(v) for v in sdecs]


CONST_SHAPES = None


def build_program(consts_shapes, sdec, sdecs, dbg_names=(), stop=None):
    nc = bass.Bass("TRN2", target_bir_lowering=False, dynamic_dma_scratch_size=8192)
    S = Sched(nc)
    dram = {}

    def din(name, shape, dt=F32):
        dram[name] = nc.dram_tensor(name, list(shape), dt, kind="ExternalInput").ap()
        return dram[name]

    def dout(name, shape):
        dram[name] = nc.dram_tensor(name, list(shape), F32, kind="ExternalOutput").ap()
        return dram[name]

    xp = din("xp", [T, D])
    xs_in = din("xs", [NS, D])
    cT_in = din("cT", [128, 8, 5])
    ada_w = din("ada_w", [L, D, 6 * D])
    ada_bT = din("ada_bT", [128, L, 48])
    w_in = din("w_in", [L, D, NIN])
    w_oa = din("w_oa", [L, 512, D])
    w_ob = din("w_ob", [L, D, D])
    w_o = din("w_o", [L, D, D])
    ffn_w_in = din("ffn_w_in", [L, D, 2 * DFF])
    ffn_w_out = din("ffn_w_out", [L, DFF, D])
    n1T = din("n1T", [128, L, 8])
    n2T = din("n2T", [128, L, 8])
    cwT = din("cwT", [128, L, 3, 22])
    cbT = din("cbT", [128, L, 22])
    gnT_in = din("gnT", [128, L, 8])
    nfw = din("nfw", [128, D])
    posT = din("posT", [128, L, 2, 32])
    w1bd = din("w1bd", [128, L, 2, 32, 128])
    w2bd = din("w2bd", [128, L, 2, 128])
    st_ret = din("st_ret", [L, 4, 4, 128, 256])
    st_conv = din("st_conv", [L, 4, 2, DFF])
    st_win = din("st_win", [L, 4, 512, 256])
    USE_CACHE = True
    if USE_CACHE:
        cache_cmp = [din("cache_cmp%d" % i, [2560 * 128, 256]) for i in range(L)]
        cache_sel = [din("cache_sel%d" % i, [2560 * 128, 256]) for i in range(L)]
    ptab = din("ptab", [128, 4, 64], I32)
    cin = {k: din("c_" + k, shp) for k, shp in consts_shapes.items()}

    y_p = dout("y_p", [T, D])
    y_s = dout("y_s", [NS, D])
    ncmp_p = dout("ncmp_p", [L, T, 256])
    ncmp_s = dout("ncmp_s", [L, NS, 256])
    nsel_p = dout("nsel_p", [L, T, 256])
    nsel_s = dout("nsel_s", [L, NS, 256])
    nwin_p = dout("nwin_p", [L, 512, 256])
    nwin_s = dout("nwin_s", [L, 4, 512, 256])
    nret_p = dout("nret_p", [L, 4, 128, 256])
    nret_s = dout("nret_s", [L, 4, 4, 128, 256])
    nconv_p = dout("nconv_p", [L, 2, DFF])
    nconv_s = dout("nconv_s", [L, 4, 2, DFF])

    bump = [8192]
    nid = [0]

    def sb(shape, dt, name=None):
        esz = 4 if dt in (F32, I32) else 2
        nbytes = int(np.prod(shape[1:])) * esz
        off = (bump[0] + 31) // 32 * 32
        bump[0] = off + nbytes
        assert bump[0] <= 229376 - 256, ("SBUF overflow", name, bump[0])
        nid[0] += 1
        h = nc.alloc_sbuf_tensor_at("%s_%d" % (name or "t", nid[0]), list(shape), dt, offset=off)
        return h

    rid = [0]

    def R(name="r"):
        rid[0] += 1
        return Res("%s%d" % (name, rid[0]))

    st = contextlib.ExitStack()
    PB = [st.enter_context(nc.psum_tensor("pb%d" % i, [128, 512], F32)) for i in range(8)]
    PBR = [R("pb") for _ in range(8)]
    pbi = [0]

    def bank():
        i = 4 + pbi[0] % 4
        pbi[0] += 1
        return PB[i], PBR[i]

    dmaq = [0]

    def dma(out, in_, reads, writes, key, q=None, nochain=False, **kw):
        if q is None:
            q = "sync"
        S.add(q, lambda e: e.dma_start(out=out, in_=in_, **kw), reads=reads, writes=writes, dma=True, key=key, nochain=nochain)

    def mm(out, lhsT, rhs, start, stop, reads, wres):
        S.add("tensor", lambda e: e.matmul(out, lhsT=lhsT, rhs=rhs, start=start, stop=stop), reads=reads, writes=[wres])

    def tr(out, in_, ident, reads, wres):
        S.add("tensor", lambda e: e.transpose(out=out, in_=in_, identity=ident), reads=reads, writes=[wres])

    alt = [0]

    def copy(out, in_, reads, writes, eng=None):
        if eng is None:
            eng = ("vector", "scalar")[alt[0] % 2]
            alt[0] += 1
        if eng == "scalar":
            S.add("scalar", lambda e: e.copy(out=out, in_=in_), reads=reads, writes=writes)
        elif eng == "vector":
            S.add("vector", lambda e: e.tensor_copy(out=out, in_=in_), reads=reads, writes=writes)
        else:
            S.add("gpsimd", lambda e: e.tensor_copy(out=out, in_=in_), reads=reads, writes=writes)

    def act(out, in_, func, reads, writes, bias=None, scale=None, accum_out=None):
        kw = {}
        if bias is not None:
            kw["bias"] = bias
        if scale is not None:
            kw["scale"] = scale
        if accum_out is not None:
            kw["accum_out"] = accum_out
        S.add("scalar", lambda e: e.activation(out=out, in_=in_, func=func, **kw), reads=reads, writes=writes)

    def tt(out, in0, in1, op, reads, writes, eng="vector"):
        S.add(eng, lambda e: e.tensor_tensor(out=out, in0=in0, in1=in1, op=op), reads=reads, writes=writes)

    def ts(out, in0, s1, s2, op0, op1, reads, writes, eng="vector"):
        if op1 is None:
            S.add(eng, lambda e: e.tensor_scalar(out=out, in0=in0, scalar1=s1, scalar2=None, op0=op0), reads=reads, writes=writes)
        else:
            S.add(eng, lambda e: e.tensor_scalar(out=out, in0=in0, scalar1=s1, scalar2=s2, op0=op0, op1=op1), reads=reads, writes=writes)

    def stt(out, in0, scalar, in1, op0, op1, reads, writes, eng="vector"):
        S.add(eng, lambda e: e.scalar_tensor_tensor(out=out, in0=in0, scalar=scalar, in1=in1, op0=op0, op1=op1), reads=reads, writes=writes)

    def memset(ap, val, writes, eng="gpsimd"):
        S.add(eng, lambda e: e.memset(ap, val), writes=writes)

    dbg_out = {}

    def dbg(name, ap, res, shape, ci, l):
        if name not in dbg_names or l != 0 or ci > 3:
            return
        if name not in dbg_out:
            dbg_out[name] = dout("dbg_" + name, [4] + list(shape))
        dma(dbg_out[name][ci], ap, [res], [], "dbg_" + name, q="gpsimd")

    xT = sb([128, 8, NT], F32, "xT")
    xT_r = [R("xT") for _ in range(5)]
    ident = sb([128, 128], F32, "ident")
    identb = sb([128, 128], BF16, "identb")
    onesb = sb([128, 128], BF16, "onesb")
    r_c = R("consts")
    eall = sb([128, 2048], BF16, "eall")
    wmask = sb([128, 8, 512], BF16, "wmask")
    dmT = sb([128, 4, 128], F32, "dmT")
    qdec = sb([128, 4, 128], F32, "qdec")
    kdec = sb([128, 4], F32, "kdec")
    dmTs = sb([128, 4, 16], F32, "dmTs")
    qdecs = sb([128, 4, 16], F32, "qdecs")
    kdecs = sb([128, 4], F32, "kdecs")
    seqmask = sb([128, 4], F32, "seqmask")
    cT = sb([128, 8, 5], F32, "cT")
    scT = sb([128, 8, 5], BF16, "scT")
    modT = sb([128, 48, 5], F32, "modT")
    r_mod = R("mod")
    adab = sb([128, L, 48], F32, "adab")
    n1s = sb([128, L, 8], F32, "n1s")
    n2s = sb([128, L, 8], F32, "n2s")
    cws = sb([128, L, 3, 22], F32, "cws")
    cbs = sb([128, L, 22], F32, "cbs")
    A1 = sb([128, 8, 5], F32, "A1")
    A2 = sb([128, 8, 5], F32, "A2")
    gnT = sb([128, L, 8], F32, "gnT")
    W2 = sb([128, 2, 128], BF16, "W2")
    posS = sb([128, 2, 32], BF16, "posS")
    hb = sb([128, 2], F32, "hb")
    r_cw = R("cmpw")
    mark_kv = bump[0]
    Ss = sb([128, 4, 4, 256], F32, "Ss")
    bump[0] = mark_kv
    KVcT = sb([128, 2, T], BF16, "KVcT")
    r_KVc = R("KVc")
    KselT = sb([128, T], BF16, "KselT")
    r_Ksel = R("Ksel")
    KwinT = sb([128, T], BF16, "KwinT")
    r_Kwin = R("Kwin")
    mark_v = bump[0]
    sOS = sb([128, 3, 512], F32, "sOS")
    r_sOS = R("sOS")
    bump[0] = mark_v
    Vsel = sb([128, 16, 2, 65], BF16, "Vsel")
    r_Vsel = R("Vsel")
    Vwin = sb([128, 16, 2, 65], BF16, "Vwin")
    r_Vwin = R("Vwin")
    bump[0] = max(bump[0], mark_v + 3 * 512 * 4 + 64)
    kcT = sb([128, 128], BF16, "kcT")
    r_kc = R("kc")
    GHv = sb([128, 128], BF16, "GHv")
    r_GHv = R("GHv")
    RC = sb([128, 2, 97], BF16, "RC")
    r_RC = R("RC")
    Sst = sb([128, 4, 256], F32, "Sst")
    Sbf = sb([128, 4, 256], BF16, "Sbf")
    r_S = R("S")
    aprev = sb([128, 22, 2], F32, "aprev")
    r_aprev = R("aprev")
    r_Ss = R("Ss")
    aprev_s = sb([128, 4, 22, 2], F32, "aprev_s")
    r_aprev_s = R("aprev_s")
    NWB = 2
    WB = [sb([128, 4096], BF16, "wb%d" % i) for i in range(NWB)]
    WBR = [R("wb") for _ in range(NWB)]
    wbi = [0]

    r_wsc = R("wscratch")

    def load_w(src_ap, kc, ncols, cast=False):
        i = wbi[0] % NWB
        wbi[0] += 1
        view = WB[i][:, 0:kc * ncols].rearrange("p (k n) -> p k n", k=kc)
        if cast:
            dma(view, src_ap.rearrange("(k p) n -> p k n", p=128), [], [WBR[i]], "wb%d" % i, q="gpsimd")
        else:
            dma(view, src_ap.rearrange("(k p) n -> p k n", p=128), [r_wsc], [WBR[i]], "wb%d" % i, q="sync")
        return view, WBR[i]

    hT = sb([128, 8, 512], BF16, "hT")
    r_hT = R("hT")
    sq = sb([128, 512], BF16, "sq")
    r_sq = R("sq")
    rstd = sb([128, 512], F32, "rstd")
    r_rstd = R("rstd")
    xn = sb([128, 512], F32, "xn")
    r_xn = R("xn")
    cosF = sb([128, 512], F32, "cosF")
    sinF = sb([128, 512], F32, "sinF")
    r_rope = R("rope")
    cosTk = sb([128, 4, 64], F32, "cosTk")
    sinTk = sb([128, 4, 64], F32, "sinTk")
    cmask = sb([128, 512], BF16, "cmask")
    sokc = sb([128, 4, 32], F32, "sokc")
    addcc = sb([128, 4, 32], F32, "addcc")
    r_cc = R("chunkconst")
    uu = sb([128, 512], F32, "uu")
    r_uu = R("uu")
    u2 = sb([128, 512], F32, "u2")
    r_u2 = R("u2")
    mark_u = bump[0]
    gates = sb([128, 4, 24], F32, "gates")
    r_gates = R("gates")
    tok32 = sb([128, 512], F32, "tok32")
    r_tok32 = R("tok32")
    onsaT = sb([128, 4, 512], BF16, "onsaT")
    r_onsaT = R("onsaT")
    Ssb1 = sb([128, 256], BF16, "Ssb1")
    r_Ssb1 = R("Ssb1")
    bnst = sb([128, 8], F32, "bnst")
    r_bn = R("bn")
    zr = sb([128, 8], F32, "zr")
    r_zr = R("zr")
    mark_late = bump[0]
    roT = sb([128, 8, 512], BF16, "roT")
    r_roT = R("roT")
    mixT = sb([128, 8, 512], BF16, "mixT")
    r_mixT = R("mixT")
    mark_late_end = bump[0]
    bump[0] = mark_late
    W1 = sb([128, 2, 32, 128], BF16, "W1")
    bump[0] = max(bump[0], mark_late_end)
    mark_nsa = bump[0]
    QT = sb([128, 4, 512], BF16, "QT")
    r_QT = R("QT")
    oacc = sb([128, 4, 512], F32, "oacc")
    r_oacc = R("oacc")
    mark_nsa2 = bump[0]
    ET = [sb([128, 512], BF16, "ET%d" % i) for i in range(3)]
    r_ET = [R("ET") for _ in range(3)]
    eti = [0]
    IMP = sb([128, 4, 2, 32], F32, "IMP")
    r_IMP = R("IMP")
    cmp3 = sb([128, 32, 32], F32, "cmp3")
    r_cmp3 = R("cmp3")
    score = sb([128, 32], F32, "score")
    rank = sb([128, 96], F32, "rank")
    r_score = R("score")
    BiasT = sb([128, 2, 512], BF16, "BiasT")
    r_BiasT = R("BiasT")
    mark_nsa_end = bump[0]
    bump[0] = mark_nsa2
    sPG = sb([128, 4, 256], F32, "sPG")
    r_sPG = R("sPG")
    mk = bump[0]
    sKV = sb([128, 2, 528], BF16, "sKV")
    r_sKV = R("sKV")
    bump[0] = mk
    sWinB = sb([128, 512], F32, "sWinB")
    bump[0] = max(bump[0], mk + 2112)
    mk = bump[0]
    sHid = sb([128, 2, 512], F32, "sHid")
    r_sHid = R("sHid")
    bump[0] = mk
    sSc = sb([128, 512], F32, "sSc")
    r_sSc = r_sHid
    bump[0] = mk + 4096
    skc = sb([128, 512], BF16, "skc")
    r_skc = R("skc")
    sGHv = sb([128, 512], BF16, "sGHv")
    r_sGHv = R("sGHv")
    sRC = sb([128, 4, 2, 194], BF16, "sRC")
    r_sRC = R("sRC")
    sKT = sb([128, 512], BF16, "sKT")
    r_sKT = R("sKT")
    sV = sb([128, 4, 2, 65], BF16, "sV")
    sQ = sb([128, 16], BF16, "sQ")
    r_sQ = R("sQ")
    r_sV = R("sV")
    sPT = sb([128, 64], BF16, "sPT")
    r_sPT = R("sPT")
    sIdx = sb([128, 64], I32, "sIdx")
    sIota = sb([128, 1], I32, "sIota")
    r_sIdx = R("sIdx")
    sZ = sb([128, 24], F32, "sZ")
    r_sZ = R("sZ")
    sRes = sb([128, 194], F32, "sRes")
    r_sRes = R("sRes")
    sImp = sb([128, 2, 129], F32, "sImp")
    r_sImp = R("sImp")
    sRank = sb([128, 136], F32, "sRank")
    sB16 = sb([128, 128], F32, "sB16")
    r_sB16 = R("sB16")
    sKn = sb([128, 2, 16], BF16, "sKn")
    sVn = sb([128, 2, 2, 65], BF16, "sVn")
    r_sNew = R("sNew")
    sNewM = sb([128, 4, 16], F32, "sNewM")
    sAddc = sb([128, 129], F32, "sAddc")
    sSelm = sb([128, 4], F32, "sSelm")
    sSelmT = sb([128, 16], F32, "sSelmT")
    r_sC = R("sC")
    mark_samp_end = bump[0]
    bump[0] = mark_nsa
    rqdT = sb([128, 4, 512], BF16, "rqdT")
    rkT = sb([128, 4, 512], BF16, "rkT")
    r_rq = R("rq")
    r_rk = R("rk")
    rope1 = sb([128, 512], F32, "rope1")
    rope2 = sb([128, 512], F32, "rope2")
    r_rope1 = R("rope1")
    r_rope2 = R("rope2")
    gsig = rope1
    ytmp = rope2
    r_gsig = r_rope1
    r_ytmp = r_rope2
    rk_tok = sb([128, 4, 512], BF16, "rk_tok")
    r_rktok = R("rktok")
    rv_tok = sb([128, 4, 1024], BF16, "rv_tok")
    r_rvtok = R("rvtok")
    rg_tok = sb([128, 4, 1024], BF16, "rg_tok")
    r_rgtok = R("rgtok")
    innT = sb([128, 128], BF16, "innT")
    r_innT = R("innT")
    ro32 = sb([128, 256], F32, "ro32")
    r_ro32 = R("ro32")
    bump[0] = max(bump[0], mark_nsa_end, mark_samp_end)
    mark_end_mixer = bump[0]
    bump[0] = mark_u
    gT = sb([128, 22, 512], BF16, "gT")
    r_gT = R("gT")
    aext = sb([128, 516], F32, "aext")
    r_aext = R("aext")
    convo = sb([128, 22, 2], F32, "convo")
    r_convo = R("convo")
    xtok = sb([128, 1024], F32, "xtok")
    r_xtok = R("xtok")
    nfws = sb([128, 1024], F32, "nfws")
    r_nfws = R("nfws")
    fin2 = sb([128, 1024], F32, "fin2")
    r_fin2 = R("fin2")
    bump[0] = max(bump[0], mark_end_mixer)
    print("SBUF bytes/partition used:", bump[0], "mark_u", mark_u, "mixer_end", mark_end_mixer)

    def ld(dst, src, res, key, q="sync"):
        dma(dst, src, [], [res], key, q=q)

    ld(ident[:], cin["ident"], r_c, "c0")
    ld(identb[:], cin["ident"], r_c, "c0", q="gpsimd")
    ld(eall[:], cin["eall"], r_c, "c0", q="gpsimd")
    ld(wmask[:], cin["wmask"].rearrange("i p n -> p i n"), r_c, "c0", q="gpsimd")
    ld(dmT[:], cin["dmT"], r_c, "c0")
    ld(qdec[:], cin["qdec"], r_c, "c0")
    ld(kdec[:], cin["kdec"], r_c, "c0")
    ld(dmTs[:], cin["dmTs"], r_c, "c0")
    ld(qdecs[:], cin["qdecs"], r_c, "c0")
    ld(kdecs[:], cin["kdecs"], r_c, "c0")
    ld(seqmask[:], cin["seqmask"], r_c, "c0")
    ld(cT[:], cT_in, r_c, "c0")
    ld(adab[:], ada_bT, r_c, "c0")
    ld(n1s[:], n1T, r_c, "c0")
    ld(gnT[:], gnT_in, r_c, "c0")
    ld(n2s[:], n2T, r_c, "c0")
    ld(cws[:], cwT, r_c, "c0")
    ld(cbs[:], cbT, r_c, "c0")
    memset(onesb[:], 1.0 / 1024.0, [r_c])
    memset(Vsel[:, :, :, 64:65], 1.0, [r_Vsel])
    memset(Vwin[:, :, :, 64:65], 1.0, [r_Vwin])
    act(scT[:], cT[:], AF.Silu, [r_c], [r_c])

    for tt_i in range(17):
        ntk = 128 if tt_i < 16 else 16
        src = xp[tt_i * 128:(tt_i + 1) * 128, :] if tt_i < 16 else xs_in
        dma(xtok[0:ntk, :], src, [], [r_xtok], "xtok")
        for half in range(2):
            pb, pr = bank()
            for k4 in range(4):
                k = half * 4 + k4
                tr(pb[:, k4 * 128:k4 * 128 + ntk], xtok[0:ntk, k * 128:(k + 1) * 128], ident[0:ntk, 0:ntk], [r_xtok, r_c], pr)
            ch = min(tt_i // 4, 4)
            copy(xT[:, half * 4:half * 4 + 4, tt_i * 128:tt_i * 128 + ntk],
                 pb[:].rearrange("p (a b) -> p a b", a=4)[:, :, 0:ntk], [pr], [xT_r[ch]])

    class _Stop(Exception):
        pass

    def chk(name):
        if stop == name:
            raise _Stop()

    def scratch(name, shape):
        return nc.dram_tensor(name, list(shape), BF16, kind="Internal").ap()

    def precast(src, dst, rows, cols, bc):
        kc = rows // 128
        for l_ in range(L):
            c0 = 0
            while c0 < cols:
                ncl = min(bc, cols - c0)
                wv_, wr_ = load_w(src[l_, :, c0:c0 + ncl], kc, ncl, cast=True)
                dma(dst[l_, :, c0:c0 + ncl].rearrange("(k p) n -> p k n", p=128), wv_, [wr_], [r_wsc], "precast", nochain=True)
                c0 += ncl

    wb_in = scratch("wb_in", [L, D, NIN])
    wb_oa = scratch("wb_oa", [L, 512, D])
    wb_ob = scratch("wb_ob", [L, D, D])
    wb_o = scratch("wb_o", [L, D, D])
    wb_fin = scratch("wb_fin", [L, D, 2 * DFF])
    wb_fout = scratch("wb_fout", [L, DFF, D])
    precast(w_in, wb_in, D, NIN, 512)
    precast(w_oa, wb_oa, 512, D, 1024)
    precast(w_ob, wb_ob, D, D, 512)
    precast(w_o, wb_o, D, D, 512)
    precast(ffn_w_in, wb_fin, D, 2 * DFF, 512)
    precast(ffn_w_out, wb_fout, DFF, D, 128)
    w_in, w_oa, w_ob, w_o, ffn_w_in, ffn_w_out = wb_in, wb_oa, wb_ob, wb_o, wb_fin, wb_fout

    chunks = [dict(t0=c * 512, n=512, kind="p", idx=c, ntile=4) for c in range(4)]
    chunks.append(dict(t0=T, n=NS, kind="s", idx=4, ntile=1))

    def rmsnorm_to_hT(ch, A, Bcol0):
        t0, n = ch["t0"], ch["n"]
        xr = xT_r[ch["idx"]]
        pb, pr = bank()
        for k in range(8):
            act(sq[:, 0:n], xT[:, k, t0:t0 + n], AF.Square, [xr], [r_sq])
            mm(pb[:, 0:n], onesb[:], sq[:, 0:n], k == 0, k == 7, [r_sq, r_c], pr)
        act(rstd[:, 0:n], pb[:, 0:n], AF.Sqrt, [pr], [r_rstd], bias=1e-6)
        S.add("vector", lambda e: e.reciprocal(out=rstd[:, 0:n], in_=rstd[:, 0:n]), reads=[r_rstd], writes=[r_rstd])
        for k in range(8):
            tt(xn[:, 0:n], xT[:, k, t0:t0 + n], rstd[:, 0:n], ALU.mult, [xr, r_rstd], [r_xn])
            if ch["kind"] == "p":
                act(hT[:, k, 0:n], xn[:, 0:n], AF.Identity, [r_xn, r_mod], [r_hT],
                    bias=modT[:, Bcol0 + k, 0:1], scale=A[:, k, 0:1])
            else:
                for s in range(4):
                    act(hT[:, k, 4 * s:4 * s + 4], xn[:, 4 * s:4 * s + 4], AF.Identity, [r_xn, r_mod], [r_hT],
                        bias=modT[:, Bcol0 + k, s + 1:s + 2], scale=A[:, k, s + 1:s + 2])

    def fm_proj(wv, wr, c0, m, kc, rhs_fn, n, rreads):
        pb, pr = bank()
        for k in range(kc):
            mm(pb[0:m, 0:n], wv[:, k, c0:c0 + m], rhs_fn(k), k == 0, k == kc - 1, [wr] + rreads, pr)
        return pb, pr

    def tm_proj(wv, wr, c0, ncols, kc, lhs_fn, ntk, lreads):
        pb, pr = bank()
        for k in range(kc):
            mm(pb[0:ntk, 0:ncols], lhs_fn(k), wv[:, k, c0:c0 + ncols], k == 0, k == kc - 1, [wr] + lreads, pr)
        return pb, pr

    def residual_add(ch, f, pb, pr, gcol0):
        t0, n = ch["t0"], ch["n"]
        xr = xT_r[ch["idx"]]
        if ch["kind"] == "p":
            stt(xT[:, f, t0:t0 + n], pb[:, 0:n], modT[:, gcol0 + f, 0:1], xT[:, f, t0:t0 + n], ALU.mult, ALU.add,
                [pr, r_mod, xr], [xr])
        else:
            for s in range(4):
                stt(xT[:, f, t0 + 4 * s:t0 + 4 * s + 4], pb[:, 4 * s:4 * s + 4], modT[:, gcol0 + f, s + 1:s + 2],
                    xT[:, f, t0 + 4 * s:t0 + 4 * s + 4], ALU.mult, ALU.add, [pr, r_mod, xr], [xr])

    try:
        chk("setup")
        for l in range(L):
            for blk in range(12):
                wv, wr = load_w(ada_w[l, :, blk * 512:(blk + 1) * 512], 8, 512, cast=True)
                pb, pr = bank()
                for j in range(4):
                    for k in range(8):
                        mm(pb[:, j * 8:j * 8 + 5], wv[:, k, j * 128:(j + 1) * 128], scT[:, k, :], k == 0, k == 7, [wr, r_c], pr)
                for j in range(4):
                    jj = blk * 4 + j
                    ts(modT[:, jj, :], pb[:, j * 8:j * 8 + 5], adab[:, l, jj:jj + 1], None, ALU.add, None, [pr, r_c], [r_mod])
            for k in range(8):
                ts(A1[:, k, :], modT[:, 8 + k, :], 1.0, n1s[:, l, k:k + 1], ALU.add, ALU.mult, [r_mod, r_c], [r_mod])
                ts(A2[:, k, :], modT[:, 32 + k, :], 1.0, n2s[:, l, k:k + 1], ALU.add, ALU.mult, [r_mod, r_c], [r_mod])
            chk("ada")
            dma(W2[:], w2bd[:, l], [], [r_cw], "cw", q="gpsimd")
            dma(posS[:], posT[:, l], [], [r_cw], "cw", q="gpsimd")
            memset(kcT[:], 0.0, [r_kc])
            memset(Vsel[:, :, :, 64:65], 1.0, [r_Vsel])
            memset(Vwin[:, :, :, 64:65], 1.0, [r_Vwin])
            memset(GHv[:], 0.0, [r_GHv])
            memset(Sst[:], 0.0, [r_S])
            memset(aprev[:], 0.0, [r_aprev])
            for s in range(4):
                for j in range(2):
                    dma(aprev_s[:, s, :, j], st_conv[l, s, j].rearrange("(f p) -> p f", p=128), [], [r_aprev_s], "aprev_s",
                        allow_slow_non_contiguous=True)
            memset(Sbf[:], 0.0, [r_S])

            for ch in chunks:
                t0, n, kind, ci = ch["t0"], ch["n"], ch["kind"], ch["idx"]
                ntile = ch["ntile"]
                ntk = 128 if kind == "p" else NS
                S.barrier()
                dma(cosF[:, 0:n], cin["cosF"][:, t0:t0 + n], [], [r_rope], "rope")
                dma(sinF[:, 0:n], cin["sinF"][:, t0:t0 + n], [], [r_rope], "rope")
                tl0 = t0 // 128
                dma(cosTk[:, 0:ntile], cin["cosT"][:, tl0:tl0 + ntile], [], [r_rope], "rope")
                dma(sinTk[:, 0:ntile], cin["sinT"][:, tl0:tl0 + ntile], [], [r_rope], "rope")
                r_W1 = R("W1")
                for e_ in range(2):
                    for lh in range(2):
                        dma(W1[:, e_, lh * 16:(lh + 1) * 16, :], w1bd[:, l, e_, lh * 16:(lh + 1) * 16, :], [], [r_W1], "W1", q="gpsimd", nochain=True)
                if kind == "p":
                    for e_ in range(0):
                        for lh in range(2):
                            dma(W1[:, e_, lh * 16:(lh + 1) * 16, :], w1bd[:, l, e_, lh * 16:(lh + 1) * 16, :], [], [r_W1], "W1", q="gpsimd", nochain=True)
                    if ci == 0:
                        for e_ in range(2):
                            pb, pr = bank()
                            for li in range(32):
                                mm(pb[:, 0:1], W1[:, e_, li, :], posS[:, e_, li:li + 1], li == 0, li == 31, [r_cw, r_W1], pr)
                            copy(hb[:, e_:e_ + 1], pb[:, 0:1], [pr], [r_cw], eng="vector")
                    dma(cmask[:], cin["cmask"][ci], [], [r_cc], "cc", q="gpsimd")
                    dma(sokc[:], cin["sok"][:, 4 * ci:4 * ci + 4], [], [r_cc], "cc")
                    dma(addcc[:], cin["addc"][:, 4 * ci:4 * ci + 4], [], [r_cc], "cc")
                if kind == "s":
                    for s in range(4):
                        dma(Ss[:, s], st_ret[l, s].rearrange("h k v -> k h v"), [], [r_Ss], "Ss")
                rmsnorm_to_hT(ch, A1, 0)
                chk("norm1")
                hk = lambda k: hT[:, k, 0:n]
                htk = lambda ti: (lambda k: hT[:, k, ti * 128:ti * 128 + ntk])

                wv, wr = load_w(w_in[l, :, 0:512], 8, 512)
                for cq in range(4):
                    pb, pr = bank()
                    for g in range(2):
                        hcol = (4 * g + cq) * 64
                        for k in range(8):
                            mm(pb[g * 64:(g + 1) * 64, 0:n], wv[:, k, hcol:hcol + 64], hk(k), k == 0, k == 7, [wr, r_hT], pr)
                    act(QT[:, cq, 0:n], pb[:, 0:n], AF.Identity, [pr], [r_QT], scale=0.125)
                wv, wr = load_w(w_in[l, :, 512:1024], 8, 512)
                for e_ in range(2):
                    pb, pr = fm_proj(wv, wr, e_ * 128, 128, 8, hk, n, [r_hT])
                    if kind == "p":
                        copy(KVcT[:, e_, t0:t0 + n], pb[:, 0:n], [pr], [r_KVc])
                pb, pr = fm_proj(wv, wr, 256, 128, 8, hk, n, [r_hT])
                if kind == "p":
                    copy(KselT[:, t0:t0 + n], pb[:, 0:n], [pr], [r_Ksel])
                else:
                    copy(sKn[:, 0, :], pb[:, 0:16], [pr], [r_sNew], eng="vector")
                for ti in range(ntile):
                    pb, pr = tm_proj(wv, wr, 0, 512, 8, htk(ti), ntk, [r_hT])
                    copy(tok32[0:ntk, :], pb[0:ntk, :], [pr], [r_tok32], eng="vector")
                    gt = (t0 - (0 if kind == "p" else T)) + ti * 128
                    dst_c = (ncmp_p if kind == "p" else ncmp_s)[l, gt:gt + ntk, :]
                    dst_s = (nsel_p if kind == "p" else nsel_s)[l, gt:gt + ntk, :]
                    dma(dst_c, tok32[0:ntk, 0:256], [r_tok32], [], "o_cmp")
                    dma(dst_s, tok32[0:ntk, 256:512], [r_tok32], [], "o_sel")
                    if kind == "p":
                        kt = ci * 4 + ti
                        copy(Vsel[:, kt, :, 0:64], tok32[:, 384:512].rearrange("p (g d) -> p g d", g=2), [r_tok32], [r_Vsel], eng="scalar")
                    else:
                        copy(sVn[0:16, 0, :, 0:64], tok32[0:16, 384:512].rearrange("p (g d) -> p g d", g=2), [r_tok32], [r_sNew], eng="scalar")
                wv, wr = load_w(w_in[l, :, 1024:1304], 8, 280)
                pb, pr = fm_proj(wv, wr, 0, 128, 8, hk, n, [r_hT])
                if kind == "p":
                    copy(KwinT[:, t0:t0 + n], pb[:, 0:n], [pr], [r_Kwin])
                else:
                    copy(sKn[:, 1, :], pb[:, 0:16], [pr], [r_sNew], eng="vector")
                for ti in range(ntile):
                    pb, pr = tm_proj(wv, wr, 0, 280, 8, htk(ti), ntk, [r_hT])
                    copy(tok32[0:ntk, 0:280], pb[0:ntk, 0:280], [pr], [r_tok32], eng="vector")
                    if kind == "p":
                        kt = ci * 4 + ti
                        copy(Vwin[:, kt, :, 0:64], tok32[:, 128:256].rearrange("p (g d) -> p g d", g=2), [r_tok32], [r_Vwin], eng="scalar")
                        if ci == 3:
                            dma(nwin_p[l, ti * 128:(ti + 1) * 128, :], tok32[:, 0:256], [r_tok32], [], "o_win")
                    else:
                        copy(sVn[0:16, 1, :, 0:64], tok32[0:16, 128:256].rearrange("p (g d) -> p g d", g=2), [r_tok32], [r_sNew], eng="scalar")
                        for s in range(4):
                            dma(nwin_s[l, s, 508:512, :], tok32[4 * s:4 * s + 4, 0:256], [r_tok32], [], "o_win")
                    act(gates[0:ntk, ti, :], tok32[0:ntk, 256:280], AF.Sigmoid, [r_tok32], [r_gates])
                if kind == "s":
                    for s in range(4):
                        dma(nwin_s[l, s, 0:508, :], st_win[l, s, 4:512, :], [], [], "o_win2")

                chk("proj")
                if kind == "p":
                    n0 = max(0, 32 * ci - 1)
                    n1 = 32 * ci + 30
                    cnt = n1 - n0 + 1
                    for e_ in range(2):
                        pb, pr = bank()
                        for li in range(32):
                            rhs = KVcT[:, e_, 16 * n0 + li:16 * n0 + li + 16 * (cnt - 1) + 1:16]
                            mm(pb[:, 0:cnt], W1[:, e_, li, :], rhs, li == 0, li == 31, [r_cw, r_W1, r_KVc], pr)
                        act(uu[:, 0:cnt], pb[:, 0:cnt], AF.Identity, [pr, r_cw], [r_uu], bias=hb[:, e_:e_ + 1])
                        tt(u2[:, 0:cnt], uu[:, 0:cnt], uu[:, 0:cnt], ALU.mult, [r_uu], [r_u2])
                        ts(u2[:, 0:cnt], u2[:, 0:cnt], 0.044715, 1.0, ALU.mult, ALU.add, [r_u2], [r_u2])
                        tt(u2[:, 0:cnt], u2[:, 0:cnt], uu[:, 0:cnt], ALU.mult, [r_u2, r_uu], [r_u2])
                        act(u2[:, 0:cnt], u2[:, 0:cnt], AF.Sigmoid, [r_u2], [r_u2], scale=1.5957691216057308)
                        if e_ == 0:
                            tt(sq[:, 0:cnt], u2[:, 0:cnt], uu[:, 0:cnt], ALU.mult, [r_u2, r_uu], [r_sq])
                            pb2, pr2 = bank()
                            mm(pb2[:, 0:cnt], W2[:, 0, :], sq[:, 0:cnt], True, True, [r_cw, r_sq], pr2)
                            copy(kcT[:, n0:n0 + cnt], pb2[:, 0:cnt], [pr2], [r_kc], eng="vector")
                        else:
                            tt(GHv[:, n0:n0 + cnt], u2[:, 0:cnt], uu[:, 0:cnt], ALU.mult, [r_u2, r_uu], [r_GHv])
                            pb2, pr2 = bank()
                            mm(pb2[:, 0:128], GHv[:, :], W2[:, 1, :], True, True, [r_cw, r_GHv], pr2)
                            copy(RC[:, :, 33:97], pb2[:, 0:128].rearrange("p (g d) -> p g d", g=2), [pr2], [r_RC], eng="vector")
                    if ci == 0:
                        for g in range(2):
                            dma(RC[:, g, 0:33], cin["rcc"], [], [r_RC], "rcc", q="gpsimd")
                    memset(IMP[:], 0.0, [r_IMP], eng="vector")
                    for h in range(8):
                        g, cq = h // 4, h % 4
                        pb, pr = bank()
                        mm(pb[:, :], kcT[g * 64:(g + 1) * 64, :], QT[g * 64:(g + 1) * 64, cq, :], True, False, [r_kc, r_QT], pr)
                        mm(pb[:, :], identb[:], cmask[:], False, True, [r_c, r_cc], pr)
                        ei = eti[0] % 3
                        eti[0] += 1
                        act(ET[ei][:], pb[:], AF.Exp, [pr], [r_ET[ei]])
                        pb2, pr2 = bank()
                        for qt in range(4):
                            mm(pb2[:, qt * 128:qt * 128 + 97], ET[ei][:, qt * 128:(qt + 1) * 128], RC[:, g, :], True, True, [r_ET[ei], r_RC], pr2)
                        pv = pb2[:].rearrange("p (q c) -> p q c", q=4)
                        ts(zr[:, 0:4], pv[:, :, 32], 1e-30, None, ALU.max, None, [pr2], [r_zr])
                        S.add("vector", lambda e: e.reciprocal(out=zr[:, 0:4], in_=zr[:, 0:4]), reads=[r_zr], writes=[r_zr])
                        for qt in range(4):
                            stt(IMP[:, qt, g, :], pv[:, qt, 0:32], zr[:, qt:qt + 1], IMP[:, qt, g, :], ALU.mult, ALU.add, [pr2, r_zr, r_IMP], [r_IMP])
                        tt(zr[:, 4:8], zr[:, 0:4], gates[:, :, h * 3 + 0], ALU.mult, [r_zr, r_gates], [r_zr])
                        col = cq * 128 + g * 64
                        for qt in range(4):
                            ts(oacc[:, qt, col:col + 64], pv[:, qt, 33:97], zr[:, 4 + qt:5 + qt], None, ALU.mult, None, [pr2, r_zr], [r_oacc])
                    for qt in range(4):
                        for g in range(2):
                            tt(score[:], IMP[:, qt, g, :], sokc[:, qt, :], ALU.mult, [r_IMP, r_cc], [r_score])
                            tt(score[:], score[:], addcc[:, qt, :], ALU.add, [r_score, r_cc], [r_score])
                            tt(cmp3[:], score[:].unsqueeze(1).to_broadcast([128, 32, 32]),
                               score[:].unsqueeze(2).to_broadcast([128, 32, 32]), ALU.is_gt, [r_score], [r_cmp3])
                            S.add("vector", lambda e: e.reduce_sum(out=rank[:, 0:32], in_=cmp3[:], axis=AX.X), reads=[r_cmp3], writes=[r_score])
                            ts(rank[:, 64:96], rank[:, 0:32], 15.5, NEGB, ALU.is_gt, ALU.mult, [r_score], [r_score])
                            pb, pr = bank()
                            tr(pb[0:96, 0:128], rank[:, 0:96], ident[:], [r_score, r_c], pr)
                            copy(BiasT[0:32, g, qt * 128:(qt + 1) * 128], pb[64:96, 0:128], [pr], [r_BiasT], eng="vector")

                    def strips(KT, rK, V, rV, kts, use_bias, gate_idx, first):
                        for h in range(8):
                            g, cq = h // 4, h % 4
                            started = [False] * 4
                            for ik, kt in enumerate(kts):
                                i = kt - 4 * ci
                                pb, pr = bank()
                                mm(pb[:, :], KT[g * 64:(g + 1) * 64, kt * 128:(kt + 1) * 128], QT[g * 64:(g + 1) * 64, cq, :], True, False, [rK, r_QT], pr)
                                extras = []
                                if use_bias:
                                    extras.append((eall[0:32, kt * 128:(kt + 1) * 128], BiasT[0:32, g, :], [r_c, r_BiasT]))
                                if i >= 0 or not use_bias:
                                    extras.append((identb[:], wmask[:, i + 4, :], [r_c]))
                                for xi, (xl, xr_, xrd) in enumerate(extras):
                                    mm(pb[:, :], xl, xr_, False, xi == len(extras) - 1, xrd, pr)
                                ei = eti[0] % 3
                                eti[0] += 1
                                act(ET[ei][:], pb[:], AF.Exp, [pr], [r_ET[ei]])
                                for qt in range(4):
                                    d = qt - i
                                    if d < 0 or (not use_bias and d > 4):
                                        continue
                                    last = True
                                    for kt2 in kts[ik + 1:]:
                                        d2 = qt - (kt2 - 4 * ci)
                                        if d2 >= 0 and (use_bias or d2 <= 4):
                                            last = False
                                    mm(PB[qt][:, 0:65], ET[ei][:, qt * 128:(qt + 1) * 128], V[:, kt, g, :], not started[qt], last, [r_ET[ei], rV], PBR[qt])
                                    started[qt] = True
                            for qt in range(4):
                                ts(zr[:, qt:qt + 1], PB[qt][:, 64:65], 1e-30, None, ALU.max, None, [PBR[qt]], [r_zr])
                            S.add("vector", lambda e: e.reciprocal(out=zr[:, 0:4], in_=zr[:, 0:4]), reads=[r_zr], writes=[r_zr])
                            tt(zr[:, 4:8], zr[:, 0:4], gates[:, :, h * 3 + gate_idx], ALU.mult, [r_zr, r_gates], [r_zr])
                            col = cq * 128 + g * 64
                            for qt in range(4):
                                stt(oacc[:, qt, col:col + 64], PB[qt][:, 0:64], zr[:, 4 + qt:5 + qt], oacc[:, qt, col:col + 64],
                                    ALU.mult, ALU.add, [PBR[qt], r_zr, r_oacc], [r_oacc])

                    strips(KselT, r_Ksel, Vsel, r_Vsel, list(range(0, 4 * ci + 4)), True, 1, False)
                    strips(KwinT, r_Kwin, Vwin, r_Vwin, list(range(max(0, 4 * ci - 4), 4 * ci + 4)), False, 2, False)
                    for qt in range(4):
                        pb, pr = bank()
                        for cq in range(4):
                            tr(pb[:, cq * 128:(cq + 1) * 128], oacc[:, qt, cq * 128:(cq + 1) * 128], ident[:], [r_oacc, r_c], pr)
                        copy(onsaT[:, :, qt * 128:(qt + 1) * 128], pb[:].rearrange("p (a b) -> p a b", a=4), [pr], [r_onsaT])
                else:
                    dma(sNewM[0:16], cin["newm"], [], [r_sC], "sC")
                    dma(sAddc[0:4], cin["addcs"], [], [r_sC], "sC")
                    dma(sSelm[0:16], cin["selm"], [], [r_sC], "sC")
                    dma(sSelmT[0:4], cin["selmT"], [], [r_sC], "sC")
                    S.add("gpsimd", lambda e: e.iota(sIota[:], pattern=[[0, 1]], base=0, channel_multiplier=1), writes=[r_sC])
                    for g in range(2):
                        dma(sRC[:, :, g, 0:130], cin["ovs"].rearrange("(t p) j -> p t j", p=128), [], [r_sRC], "sRC", q="gpsimd")
                    memset(sV[:, :, :, 64:65], 1.0, [r_sV], eng="vector")
                    memset(sVn[:, :, :, 64:65], 1.0, [r_sNew], eng="vector")
                    bias16 = sq[0:16, 0:256].rearrange("p (g j) -> p g j", g=2)

                    def gather(cache_l, k0):
                        for pg in range(4):
                            S.add("gpsimd", lambda e, pg=pg, k0=k0, cache_l=cache_l: e.indirect_dma_start(
                                out=sPG[:, pg, :], out_offset=None, in_=cache_l,
                                in_offset=bass.IndirectOffsetOnAxis(ap=sIdx[:, k0 + pg:k0 + pg + 1], axis=0)),
                                reads=[r_sIdx], writes=[r_sPG], dma=True, key="sPG")

                    def kv_from_pages():
                        pbk, prk = bank()
                        for pg in range(4):
                            tr(pbk[:, pg * 128:(pg + 1) * 128], sPG[:, pg, 0:128], ident[:], [r_sPG, r_c], prk)
                        copy(sKT[:], pbk[:], [prk], [r_sKT])
                        copy(sV[:, :, :, 0:64], sPG[:, :, 128:256].rearrange("p a (g d) -> p a g d", g=2), [r_sPG], [r_sV])

                    def attend(g, KT_ap, nkeys, bias_fn, vtiles, first, last, rKT):
                        pb, pr = bank()
                        mm(pb[0:16, 0:nkeys], sQ[g * 64:(g + 1) * 64, :], KT_ap, True, True, [r_sQ, rKT], pr)
                        bias_fn(pb, pr)
                        act(sSc[0:16, 0:nkeys], sSc[0:16, 0:nkeys], AF.Exp, [r_sSc], [r_sSc])
                        nt = len(vtiles)
                        pt, prt = bank()
                        for j in range(nt):
                            nk = min(128, nkeys - j * 128)
                            tr(pt[0:nk, j * 16:(j + 1) * 16], sSc[0:16, j * 128:j * 128 + nk], ident[0:16, 0:16], [r_sSc, r_c], prt)
                        nk0 = min(128, nkeys)
                        copy(sPT[0:nk0, 0:nt * 16], pt[0:nk0, 0:nt * 16], [prt], [r_sPT], eng="vector")
                        for j in range(nt):
                            nk = min(128, nkeys - j * 128)
                            vap, vres = vtiles[j]
                            mm(PB[g][0:16, 0:65], sPT[0:nk, j * 16:(j + 1) * 16], vap, first and j == 0, last and j == nt - 1, [r_sPT, vres], PBR[g])

                    def finish(g, s, br):
                        ts(sZ[0:16, 0:1], PB[g][0:16, 64:65], 1e-30, None, ALU.max, None, [PBR[g]], [r_sZ])
                        S.add("vector", lambda e: e.reciprocal(out=sZ[0:16, 0:1], in_=sZ[0:16, 0:1]), reads=[r_sZ], writes=[r_sZ])
                        ts(sRes[0:16, 0:64], PB[g][0:16, 0:64], sZ[0:16, 0:1], None, ALU.mult, None, [PBR[g], r_sZ], [r_sRes])
                        for cq in range(4):
                            c0_ = cq * 128 + g * 64
                            dma(sOS[4 * s:4 * s + 4, br, c0_:c0_ + 64], sRes[4 * cq:4 * cq + 4, 0:64], [r_sRes], [r_sOS], "sOS")

                    def new_tile(g, s, which):
                        def bf(pb, pr):
                            tt(sSc[0:16, 0:16], pb[0:16, 0:16], sNewM[0:16, s, :], ALU.add, [pr, r_sC], [r_sSc])
                        attend(g, sKn[g * 64:(g + 1) * 64, which, :], 16, bf, [(sVn[0:16, which, g, :], r_sNew)], False, True, r_sNew)

                    for s in range(4):
                        copy(sQ[:, 0:16].rearrange("p (c i) -> p c i", c=4), QT[:, :, 4 * s:4 * s + 4], [r_QT], [r_sQ], eng="vector")
                        dma(sIdx[:], ptab[:, s, :], [], [r_sIdx], "sIdx")
                        stt(sIdx[:], sIdx[:], 128, sIota[:, 0:1].to_broadcast([128, 64]), ALU.mult, ALU.add, [r_sIdx, r_sC], [r_sIdx])
                        memset(sHid[:, :, 511:512], 0.0, [r_sHid], eng="vector")
                        for c in range(16):
                            gather(cache_cmp[l], 4 * c)
                            if c > 0:
                                copy(sKV[:, :, 0:16], sKV[:, :, 512:528], [r_sKV], [r_sKV], eng="vector")
                            for e_ in range(2):
                                pbk, prk = bank()
                                for pg in range(4):
                                    tr(pbk[:, pg * 128:(pg + 1) * 128], sPG[:, pg, e_ * 128:(e_ + 1) * 128], ident[:], [r_sPG, r_c], prk)
                                copy(sKV[:, e_, 16:528], pbk[:], [prk], [r_sKV])
                            n0 = max(0, 32 * c - 1)
                            cnt = 32 if c > 0 else 31
                            cs0 = 0 if c > 0 else 16
                            for e_ in range(2):
                                pb, pr = bank()
                                for li in range(32):
                                    mm(pb[:, 0:cnt], W1[:, e_, li, :], sKV[:, e_, cs0 + li:cs0 + li + 16 * (cnt - 1) + 1:16], li == 0, li == 31, [r_cw, r_W1, r_sKV], pr)
                                copy(sHid[:, e_, n0:n0 + cnt], pb[:, 0:cnt], [pr], [r_sHid])
                        for e_ in range(2):
                            act(uu[:, :], sHid[:, e_, :], AF.Identity, [r_sHid, r_cw], [r_uu], bias=hb[:, e_:e_ + 1])
                            tt(u2[:, :], uu[:, :], uu[:, :], ALU.mult, [r_uu], [r_u2])
                            ts(u2[:, :], u2[:, :], 0.044715, 1.0, ALU.mult, ALU.add, [r_u2], [r_u2])
                            tt(u2[:, :], u2[:, :], uu[:, :], ALU.mult, [r_u2, r_uu], [r_u2])
                            act(u2[:, :], u2[:, :], AF.Sigmoid, [r_u2], [r_u2], scale=1.5957691216057308)
                            if e_ == 0:
                                tt(sGHv[:, :], u2[:, :], uu[:, :], ALU.mult, [r_u2, r_uu], [r_sGHv])
                                pb2, pr2 = bank()
                                mm(pb2[:, 0:512], W2[:, 0, :], sGHv[:, :], True, True, [r_cw, r_sGHv], pr2)
                                copy(skc[:], pb2[:, 0:512], [pr2], [r_skc], eng="vector")
                            else:
                                tt(sGHv[:, :], u2[:, :], uu[:, :], ALU.mult, [r_u2, r_uu], [r_sGHv])
                                memset(sGHv[:, 511:512], 0.0, [r_sGHv], eng="vector")
                                for tl in range(4):
                                    pb2, pr2 = bank()
                                    mm(pb2[:, 0:128], sGHv[:, tl * 128:(tl + 1) * 128], W2[:, 1, :], True, True, [r_cw, r_sGHv], pr2)
                                    copy(sRC[:, tl, :, 130:194], pb2[:, 0:128].rearrange("p (g d) -> p g d", g=2), [pr2], [r_sRC], eng="vector")
                        for g in range(2):
                            pb, pr = bank()
                            mm(pb[0:16, 0:512], sQ[g * 64:(g + 1) * 64, :], skc[g * 64:(g + 1) * 64, :], True, True, [r_sQ, r_skc], pr)
                            act(sSc[0:16, :], pb[0:16, 0:512], AF.Exp, [pr], [r_sSc])
                            pt, prt = bank()
                            for tl in range(4):
                                tr(pt[:, tl * 16:(tl + 1) * 16], sSc[0:16, tl * 128:(tl + 1) * 128], ident[0:16, 0:16], [r_sSc, r_c], prt)
                            copy(sPT[:, 0:64], pt[:, 0:64], [prt], [r_sPT], eng="vector")
                            for tl in range(4):
                                mm(PB[g][0:16, 0:194], sPT[:, tl * 16:(tl + 1) * 16], sRC[:, tl, g, :], tl == 0, tl == 3, [r_sPT, r_sRC], PBR[g])
                            ts(sZ[0:16, 0:1], PB[g][0:16, 129:130], 1e-30, None, ALU.max, None, [PBR[g]], [r_sZ])
                            S.add("vector", lambda e: e.reciprocal(out=sZ[0:16, 0:1], in_=sZ[0:16, 0:1]), reads=[r_sZ], writes=[r_sZ])
                            ts(sRes[0:16, :], PB[g][0:16, 0:194], sZ[0:16, 0:1], None, ALU.mult, None, [PBR[g], r_sZ], [r_sRes])
                            for cq in range(4):
                                c0_ = cq * 128 + g * 64
                                dma(sOS[4 * s:4 * s + 4, 0, c0_:c0_ + 64], sRes[4 * cq:4 * cq + 4, 130:194], [r_sRes], [r_sOS], "sOS")
                            pb3, pr3 = bank()
                            mm(pb3[0:4, 0:129], sSelm[0:16, :], sRes[0:16, 0:129], True, True, [r_sC, r_sRes], pr3)
                            tt(sImp[0:4, g, :], pb3[0:4, 0:129], sAddc[0:4, :], ALU.add, [pr3, r_sC], [r_sImp])
                            for jb in range(43):
                                j0 = jb * 3
                                cv = uu[0:4, 0:387].rearrange("p (a b) -> p a b", a=3)
                                tt(cv, sImp[0:4, g, :].unsqueeze(1).to_broadcast([4, 3, 129]),
                                   sImp[0:4, g, j0:j0 + 3].unsqueeze(2).to_broadcast([4, 3, 129]), ALU.is_gt, [r_sImp], [r_uu])
                                S.add("vector", lambda e, j0=j0, cv=cv: e.reduce_sum(out=sRank[0:4, j0:j0 + 3], in_=cv, axis=AX.X), reads=[r_uu], writes=[r_sB16])
                            ts(sRank[0:4, 0:128], sRank[0:4, 0:128], 15.5, NEGB, ALU.is_gt, ALU.mult, [r_sB16], [r_sB16])
                            pb4, pr4 = bank()
                            mm(pb4[0:16, 0:128], sSelmT[0:4, :], sRank[0:4, 0:128], True, True, [r_sC, r_sB16], pr4)
                            copy(bias16[:, g, :], pb4[0:16, 0:128], [pr4], [r_sq], eng="vector")
                        for kc in range(16):
                            gather(cache_sel[l], 4 * kc)
                            kv_from_pages()
                            for g in range(2):
                                def bf(pb, pr, g=g, kc=kc):
                                    tt(sSc[0:16, :].rearrange("p (a b) -> p a b", a=8), pb[0:16, 0:512].rearrange("p (a b) -> p a b", a=8),
                                       bias16[:, g, 8 * kc:8 * kc + 8].unsqueeze(2).to_broadcast([16, 8, 64]), ALU.add, [pr, r_sq], [r_sSc])
                                attend(g, sKT[g * 64:(g + 1) * 64, :], 512, bf, [(sV[:, pg, g, :], r_sV) for pg in range(4)], kc == 0, False, r_sKT)
                        for g in range(2):
                            new_tile(g, s, 0)
                            finish(g, s, 1)
                        for pg in range(4):
                            dma(sPG[:, pg, :], st_win[l, s, pg * 128:(pg + 1) * 128, :], [], [r_sPG], "sPG")
                        kv_from_pages()
                        dma(sWinB[0:16, :], cin["winb"], [], [r_sKV], "sWinB")
                        for g in range(2):
                            def bfw(pb, pr):
                                tt(sSc[0:16, :], pb[0:16, 0:512], sWinB[0:16, :], ALU.add, [pr, r_sKV], [r_sSc])
                            attend(g, sKT[g * 64:(g + 1) * 64, :], 512, bfw, [(sV[:, pg, g, :], r_sV) for pg in range(4)], True, False, r_sKT)
                            new_tile(g, s, 1)
                            finish(g, s, 2)
                    for h in range(8):
                        g, cq = h // 4, h % 4
                        col = cq * 128 + g * 64
                        ts(oacc[0:16, 0, col:col + 64], sOS[0:16, 0, col:col + 64], gates[0:16, 0, h * 3:h * 3 + 1], None, ALU.mult, None, [r_sOS, r_gates], [r_oacc])
                        for br in (1, 2):
                            stt(oacc[0:16, 0, col:col + 64], sOS[0:16, br, col:col + 64], gates[0:16, 0, h * 3 + br:h * 3 + br + 1],
                                oacc[0:16, 0, col:col + 64], ALU.mult, ALU.add, [r_sOS, r_gates, r_oacc], [r_oacc])
                    pb, pr = bank()
                    for cq in range(4):
                        tr(pb[:, cq * 128:cq * 128 + 16], oacc[0:16, 0, cq * 128:(cq + 1) * 128], ident[0:16, 0:16], [r_oacc, r_c], pr)
                    copy(onsaT[:, :, 0:16], pb[:].rearrange("p (a b) -> p a b", a=4)[:, :, 0:16], [pr], [r_onsaT])

                if kind == "p":
                    dbg("oacc", oacc[:], r_oacc, [128, 4, 512], ci, l)
                    dbg("imp", IMP[:], r_IMP, [128, 4, 2, 32], ci, l)
                    dbg("biasT", BiasT[0:32], r_BiasT, [32, 2, 512], ci, l)
                    dbg("gates", gates[:], r_gates, [128, 4, 24], ci, l)
                    dbg("QT", QT[:], r_QT, [128, 4, 512], ci, l)
                chk("nsa")
                S.barrier()
                chk("ret_bar")
                def swap_proj(wv, wr, hd):
                    pbs, prs_ = bank()
                    for half in range(2):
                        c0s = hd * 128 + (1 - half) * 64
                        for k in range(8):
                            mm(pbs[half * 64:(half + 1) * 64, 0:n], wv[:, k, c0s:c0s + 64], hk(k), k == 0, k == 7, [wr, r_hT], prs_)
                    return pbs, prs_

                def rotary_fm(pb, pr, pbs, prs_, dstT, rdst, hd, scale):
                    tt(rope1[:, 0:n], pb[:, 0:n], cosF[:, 0:n], ALU.mult, [pr, r_rope], [r_rope1])
                    tt(rope2[:, 0:n], pbs[:, 0:n], sinF[:, 0:n], ALU.mult, [prs_, r_rope], [r_rope2])
                    stt(dstT[:, hd, 0:n], rope1[:, 0:n], scale, rope2[:, 0:n], ALU.mult, ALU.add, [r_rope1, r_rope2], [rdst])

                wv, wr = load_w(w_in[l, :, 1304:1816], 8, 512)
                for hd in range(4):
                    pb, pr = fm_proj(wv, wr, hd * 128, 128, 8, hk, n, [r_hT])
                    pbs, prs_ = swap_proj(wv, wr, hd)
                    tt(rope1[:, 0:n], pb[:, 0:n], cosF[:, 0:n], ALU.mult, [pr, r_rope], [r_rope1])
                    tt(rope2[:, 0:n], pbs[:, 0:n], sinF[:, 0:n], ALU.mult, [prs_, r_rope], [r_rope2])
                    tt(rope1[:, 0:n], rope1[:, 0:n], rope2[:, 0:n], ALU.add, [r_rope1, r_rope2], [r_rope1])
                    if kind == "p":
                        tt(rqdT[:, hd, :].rearrange("p (a c) -> p a c", a=4), rope1[:].rearrange("p (a c) -> p a c", a=4),
                           qdec[:, hd:hd + 1, :].to_broadcast([128, 4, 128]), ALU.mult, [r_rope1, r_c], [r_rq])
                    else:
                        tt(rqdT[:, hd, 0:n], rope1[:, 0:n], qdecs[:, hd, :], ALU.mult, [r_rope1, r_c], [r_rq])
                chk("ret_b3")
                wv, wr = load_w(w_in[l, :, 1816:2328], 8, 512)
                ksc = 128.0 ** -0.5
                for hd in range(4):
                    pb, pr = fm_proj(wv, wr, hd * 128, 128, 8, hk, n, [r_hT])
                    pbs, prs_ = swap_proj(wv, wr, hd)
                    rotary_fm(pb, pr, pbs, prs_, rkT, r_rk, hd, 1.0)
                    S.add("gpsimd", lambda e, hd=hd, n=n: e.tensor_scalar(out=rkT[:, hd, 0:n], in0=rkT[:, hd, 0:n], scalar1=ksc, scalar2=None, op0=ALU.mult), reads=[r_rk], writes=[r_rk])
                for ti in range(ntile):
                    pb, pr = tm_proj(wv, wr, 0, 512, 8, htk(ti), ntk, [r_hT])
                    copy(tok32[0:ntk, :], pb[0:ntk, :], [pr], [r_tok32], eng="scalar")
                    tv = tok32[0:ntk, :].rearrange("p (h two i) -> p h two i", h=4, two=2)
                    cosb = cosTk[0:ntk, ti:ti + 1, :].to_broadcast([ntk, 4, 64])
                    sinb = sinTk[0:ntk, ti:ti + 1, :].to_broadcast([ntk, 4, 64])
                    xv = xn[0:ntk, :].rearrange("p (h two i) -> p h two i", h=4, two=2)
                    tt(xv[:, :, 0, :], tv[:, :, 0, :], cosb, ALU.mult, [r_tok32, r_rope], [r_xn])
                    tt(uu[0:ntk, 0:256].rearrange("p (h i) -> p h i", h=4), tv[:, :, 1, :], sinb, ALU.mult, [r_tok32, r_rope], [r_uu])
                    tt(xv[:, :, 0, :], xv[:, :, 0, :], uu[0:ntk, 0:256].rearrange("p (h i) -> p h i", h=4), ALU.subtract, [r_xn, r_uu], [r_xn])
                    tt(xv[:, :, 1, :], tv[:, :, 0, :], sinb, ALU.mult, [r_tok32, r_rope], [r_xn])
                    tt(uu[0:ntk, 0:256].rearrange("p (h i) -> p h i", h=4), tv[:, :, 1, :], cosb, ALU.mult, [r_tok32, r_rope], [r_uu])
                    tt(xv[:, :, 1, :], xv[:, :, 1, :], uu[0:ntk, 0:256].rearrange("p (h i) -> p h i", h=4), ALU.add, [r_xn, r_uu], [r_xn])
                    kd = kdec if kind == "p" else kdecs
                    for hd in range(4):
                        ts(rk_tok[0:ntk, ti, hd * 128:(hd + 1) * 128], xn[0:ntk, hd * 128:(hd + 1) * 128], kd[0:ntk, hd:hd + 1], ksc,
                           ALU.mult, ALU.mult, [r_xn, r_c], [r_rktok])
                chk("ret_b4")
                for half in range(2):
                    wv, wr = load_w(w_in[l, :, 2328 + half * 512:2328 + (half + 1) * 512], 8, 512)
                    for ti in range(ntile):
                        pb, pr = tm_proj(wv, wr, 0, 512, 8, htk(ti), ntk, [r_hT])
                        copy(rv_tok[0:ntk, ti, half * 512:(half + 1) * 512], pb[0:ntk, :], [pr], [r_rvtok])
                for half in range(2):
                    wv, wr = load_w(w_in[l, :, 3352 + half * 512:3352 + (half + 1) * 512], 8, 512)
                    for ti in range(ntile):
                        pb, pr = tm_proj(wv, wr, 0, 512, 8, htk(ti), ntk, [r_hT])
                        act(rg_tok[0:ntk, ti, half * 512:(half + 1) * 512], pb[0:ntk, :], AF.Silu, [pr], [r_rgtok])
                chk("ret_proj")
                for ti in range(ntile):
                    c0 = ti * 128
                    for hd in range(4):
                        pb, pr = bank()
                        mm(pb[0:ntk, 0:ntk], rkT[:, hd, c0:c0 + ntk], rqdT[:, hd, c0:c0 + ntk], True, True, [r_rk, r_rq], pr)
                        dmm = dmT[:, hd, :] if kind == "p" else dmTs[0:ntk, hd, :]
                        tt(innT[0:ntk, 0:ntk], pb[0:ntk, 0:ntk], dmm[0:ntk], ALU.mult, [pr, r_c], [r_innT])
                        po, pro = bank()
                        mm(po[0:ntk, 0:256], innT[0:ntk, 0:ntk], rv_tok[0:ntk, ti, hd * 256:(hd + 1) * 256], True, False, [r_innT, r_rvtok], pro)
                        if kind == "p":
                            mm(po[0:ntk, 0:256], rqdT[:, hd, c0:c0 + ntk], Sbf[:, hd, :], False, True, [r_rq, r_S], pro)
                        else:
                            for s in range(4):
                                pass
                            memset(sq[:, 0:64], 0.0, [r_sq], eng="vector")
                            for s in range(4):
                                copy(sq[:, s * 16 + 4 * s:s * 16 + 4 * s + 4], rqdT[:, hd, 4 * s:4 * s + 4], [r_rq], [r_sq], eng="vector")
                            for s in range(4):
                                copy(Ssb1[:], Ss[:, s, hd, :], [r_Ss], [r_Ssb1], eng="scalar")
                                mm(po[0:ntk, 0:256], sq[:, s * 16:(s + 1) * 16], Ssb1[:], False, s == 3, [r_sq, r_Ssb1], pro)
                        S.add("vector", lambda e, po=po, ntk=ntk: e.bn_stats(out=bnst[0:ntk, 0:6], in_=po[0:ntk, 0:256]), reads=[pro], writes=[r_bn])
                        S.add("vector", lambda e, ntk=ntk: e.bn_aggr(out=bnst[0:ntk, 6:8], in_=bnst[0:ntk, 0:6]), reads=[r_bn], writes=[r_bn])
                        act(bnst[0:ntk, 7:8], bnst[0:ntk, 7:8], AF.Sqrt, [r_bn], [r_bn], bias=1e-5)
                        S.add("vector", lambda e, ntk=ntk: e.reciprocal(out=bnst[0:ntk, 7:8], in_=bnst[0:ntk, 7:8]), reads=[r_bn], writes=[r_bn])
                        rs = ro32[0:ntk, 0:256]
                        ts(rs, po[0:ntk, 0:256], bnst[0:ntk, 6:7], bnst[0:ntk, 7:8], ALU.subtract, ALU.mult, [pro, r_bn], [r_ro32])
                        tt(rs, rs, rg_tok[0:ntk, ti, hd * 256:(hd + 1) * 256], ALU.mult, [r_ro32, r_rgtok], [r_ro32], eng="gpsimd")
                        pbt, prt = bank()
                        for j2 in range(2):
                            tr(pbt[:, j2 * 128:j2 * 128 + ntk], ro32[0:ntk, j2 * 128:(j2 + 1) * 128], ident[0:ntk, 0:ntk], [r_ro32, r_c], prt)
                        for j2 in range(2):
                            k = 2 * hd + j2
                            ts(roT[:, k, c0:c0 + ntk], pbt[:, j2 * 128:j2 * 128 + ntk], gnT[:, l, k:k + 1], None, ALU.mult, None, [prt, r_c], [r_roT])
                        if kind == "p":
                            ps_, prs = bank()
                            mm(ps_[:, 0:256], rk_tok[:, ti, hd * 128:(hd + 1) * 128], rv_tok[:, ti, hd * 256:(hd + 1) * 256], True, True, [r_rktok, r_rvtok], prs)
                            stt(Sst[:, hd, :], Sst[:, hd, :], sdec[hd], ps_[:, 0:256], ALU.mult, ALU.add, [r_S, prs], [r_S])
                            copy(Sbf[:, hd, :], Sst[:, hd, :], [r_S], [r_S], eng="scalar")
                        else:
                            for s in range(4):
                                ts(innT[0:ntk, 0:128], rk_tok[0:ntk, 0, hd * 128:(hd + 1) * 128], seqmask[0:ntk, s:s + 1], None, ALU.mult, None, [r_rktok, r_c], [r_innT])
                                ps_, prs = bank()
                                mm(ps_[:, 0:256], innT[0:ntk, 0:128], rv_tok[0:ntk, 0, hd * 256:(hd + 1) * 256], True, True, [r_innT, r_rvtok], prs)
                                stt(Ss[:, s, hd, :], Ss[:, s, hd, :], sdecs[hd], ps_[:, 0:256], ALU.mult, ALU.add, [r_Ss, prs], [r_Ss])
                if kind == "p" and ci == 3:
                    dma(nret_p[l].rearrange("h k v -> k h v"), Sst[:], [r_S], [], "o_ret")
                if kind == "s":
                    for s in range(4):
                        dma(nret_s[l, s].rearrange("h k v -> k h v"), Ss[:, s], [r_Ss], [], "o_ret")

                if kind == "p":
                    dbg("roT", roT[:], r_roT, [128, 8, 512], ci, l)
                chk("ret")
                S.barrier()
                woa_v = []
                for f in range(8):
                    wv, wr = load_w(w_in[l, :, 4376 + f * 128:4376 + (f + 1) * 128], 8, 128)
                    pg, prg = fm_proj(wv, wr, 0, 128, 8, hk, n, [r_hT])
                    act(gsig[:, 0:n], pg[:, 0:n], AF.Sigmoid, [prg], [r_gsig])
                    i = wbi[0] % NWB
                    wbi[0] += 1
                    wva = WB[i][:, 0:512].rearrange("p (k n) -> p k n", k=4)
                    for g in range(2):
                        dma(wva[g * 64:(g + 1) * 64, :, :], w_oa[l, g * 256:(g + 1) * 256, f * 128:(f + 1) * 128].rearrange("(c d) n -> d c n", c=4),
                            [r_wsc], [WBR[i]], "wb%d" % i, q="sync", nochain=True)
                    pa, pra = fm_proj(wva, WBR[i], 0, 128, 4, lambda k: onsaT[:, k, 0:n], n, [r_onsaT])
                    tt(ytmp[:, 0:n], pa[:, 0:n], gsig[:, 0:n], ALU.mult, [pra, r_gsig], [r_ytmp])
                    wv, wr = load_w(w_in[l, :, 5400 + f * 128:5400 + (f + 1) * 128], 8, 128)
                    pg, prg = fm_proj(wv, wr, 0, 128, 8, hk, n, [r_hT])
                    act(gsig[:, 0:n], pg[:, 0:n], AF.Sigmoid, [prg], [r_gsig])
                    wv, wr = load_w(w_ob[l, :, f * 128:(f + 1) * 128], 8, 128)
                    pbb, prb = fm_proj(wv, wr, 0, 128, 8, lambda k: roT[:, k, 0:n], n, [r_roT])
                    tt(gsig[:, 0:n], pbb[:, 0:n], gsig[:, 0:n], ALU.mult, [prb, r_gsig], [r_gsig])
                    tt(mixT[:, f, 0:n], gsig[:, 0:n], ytmp[:, 0:n], ALU.add, [r_gsig, r_ytmp], [r_mixT])
                for half in range(2):
                    wv, wr = load_w(w_o[l, :, half * 512:(half + 1) * 512], 8, 512)
                    for f4 in range(4):
                        f = half * 4 + f4
                        pb, pr = fm_proj(wv, wr, f4 * 128, 128, 8, lambda k: mixT[:, k, 0:n], n, [r_mixT])
                        residual_add(ch, f, pb, pr, 16)

                if kind == "p":
                    dbg("mixT", mixT[:], r_mixT, [128, 8, 512], ci, l)
                    dbg("x1", xT[:, :, t0:t0 + n], xT_r[ci], [128, 8, 512], ci, l)
                chk("merge")
                S.barrier()
                rmsnorm_to_hT(ch, A2, 24)
                for f in range(22):
                    wv, wr = load_w(ffn_w_in[l, :, f * 128:(f + 1) * 128], 8, 128)
                    pa, pra = fm_proj(wv, wr, 0, 128, 8, hk, n, [r_hT])
                    wv, wr = load_w(ffn_w_in[l, :, DFF + f * 128:DFF + (f + 1) * 128], 8, 128)
                    pbb, prb = fm_proj(wv, wr, 0, 128, 8, hk, n, [r_hT])
                    cw = lambda j: cws[:, l, j, f:f + 1]
                    if kind == "p":
                        copy(aext[:, 0:2], aprev[:, f, :], [r_aprev], [r_aext], eng="vector")
                        copy(aext[:, 2:2 + n], pa[:, 0:n], [pra], [r_aext], eng="scalar")
                        copy(aprev[:, f, :], aext[:, n:n + 2], [r_aext], [r_aprev], eng="vector")
                        segs = [(0, n)]
                        if ci == 3:
                            copy(convo[:, f, :], aext[:, n:n + 2], [r_aext], [r_convo], eng="vector")
                        ts(uu[:, 0:n], aext[:, 0:n], cw(0), cbs[:, l, f:f + 1], ALU.mult, ALU.add, [r_aext, r_c], [r_uu])
                        stt(uu[:, 0:n], aext[:, 1:1 + n], cw(1), uu[:, 0:n], ALU.mult, ALU.add, [r_aext, r_c, r_uu], [r_uu])
                        stt(uu[:, 0:n], aext[:, 2:2 + n], cw(2), uu[:, 0:n], ALU.mult, ALU.add, [r_aext, r_c, r_uu], [r_uu])
                    else:
                        av = aext[:, 0:24].rearrange("p (s c) -> p s c", s=4)
                        copy(av[:, :, 0:2], aprev_s[:, :, f, :], [r_aprev_s], [r_aext], eng="vector")
                        copy(av[:, :, 2:6], pa[:, 0:16].rearrange("p (s c) -> p s c", s=4), [pra], [r_aext], eng="scalar")
                        copy(convo[:, f, :], av[:, 0, 4:6], [r_aext], [r_convo], eng="vector")
                        copy(aprev_s[:, :, f, :], av[:, :, 4:6], [r_aext], [r_aprev_s], eng="vector")
                        uv = uu[:, 0:16].rearrange("p (s c) -> p s c", s=4)
                        ts(uv, av[:, :, 0:4], cw(0), cbs[:, l, f:f + 1], ALU.mult, ALU.add, [r_aext, r_c], [r_uu])
                        stt(uv, av[:, :, 1:5], cw(1), uv, ALU.mult, ALU.add, [r_aext, r_c, r_uu], [r_uu])
                        stt(uv, av[:, :, 2:6], cw(2), uv, ALU.mult, ALU.add, [r_aext, r_c, r_uu], [r_uu])
                    tt(u2[:, 0:n], uu[:, 0:n], uu[:, 0:n], ALU.mult, [r_uu], [r_u2], eng="gpsimd")
                    ts(u2[:, 0:n], u2[:, 0:n], 0.044715, 1.0, ALU.mult, ALU.add, [r_u2], [r_u2], eng="gpsimd")
                    tt(u2[:, 0:n], u2[:, 0:n], uu[:, 0:n], ALU.mult, [r_u2, r_uu], [r_u2], eng="gpsimd")
                    act(u2[:, 0:n], u2[:, 0:n], AF.Sigmoid, [r_u2], [r_u2], scale=1.5957691216057308)
                    tt(u2[:, 0:n], u2[:, 0:n], uu[:, 0:n], ALU.mult, [r_u2, r_uu], [r_u2])
                    tt(gT[:, f, 0:n], u2[:, 0:n], pbb[:, 0:n], ALU.mult, [r_u2, prb], [r_gT])
                if kind == "p" and ci == 3:
                    for j in range(2):
                        dma(nconv_p[l, j].rearrange("(f p) -> p f", p=128), convo[:, :, j], [r_convo], [], "o_conv", allow_slow_non_contiguous=True)
                if kind == "s":
                    for s in range(4):
                        for j in range(2):
                            dma(nconv_s[l, s, j].rearrange("(f p) -> p f", p=128), aprev_s[:, s, :, j], [r_aprev_s], [], "o_conv", allow_slow_non_contiguous=True)
                for f in range(8):
                    i = wbi[0] % NWB
                    wbi[0] += 1
                    wvo = WB[i][:, 0:22 * 128].rearrange("p (k n) -> p k n", k=22)
                    dma(wvo, ffn_w_out[l, :, f * 128:(f + 1) * 128].rearrange("(k p) n -> p k n", p=128), [r_wsc], [WBR[i]], "wb%d" % i, q="sync")
                    pb, pr = fm_proj(wvo, WBR[i], 0, 128, 22, lambda k: gT[:, k, 0:n], n, [r_gT])
                    residual_add(ch, f, pb, pr, 40)
                chk("chunk%d" % ci)
                chk("l%dchunk%d" % (l, ci))


    except _Stop:
        pass
    S.barrier()
    dma(nfws[:], nfw, [], [r_nfws], "nfw")
    for tt_i in range(17):
        ntk = 128 if tt_i < 16 else 16
        ch = min(tt_i // 4, 4)
        for half in range(2):
            pb, pr = bank()
            for k4 in range(4):
                k = half * 4 + k4
                tr(pb[0:ntk, k4 * 128:(k4 + 1) * 128], xT[:, k, tt_i * 128:tt_i * 128 + ntk], ident[:], [xT_r[ch], r_c], pr)
            copy(xtok[0:ntk, half * 512:(half + 1) * 512], pb[0:ntk, :], [pr], [r_xtok])
        act(fin2[0:ntk, :], xtok[0:ntk, :], AF.Square, [r_xtok], [r_fin2, r_bn], accum_out=bnst[0:ntk, 0:1])
        act(bnst[0:ntk, 1:2], bnst[0:ntk, 0:1], AF.Sqrt, [r_bn], [r_bn], bias=1e-6, scale=1.0 / 1024.0)
        S.add("vector", lambda e, ntk=ntk: e.reciprocal(out=bnst[0:ntk, 1:2], in_=bnst[0:ntk, 1:2]), reads=[r_bn], writes=[r_bn])
        stt(xtok[0:ntk, :], xtok[0:ntk, :], bnst[0:ntk, 1:2], nfws[0:ntk, :], ALU.mult, ALU.mult, [r_xtok, r_bn, r_nfws], [r_xtok])
        dst = y_p[tt_i * 128:(tt_i + 1) * 128, :] if tt_i < 16 else y_s
        dma(dst, xtok[0:ntk, :], [r_xtok], [], "o_y")

    S.emit()
    st.close()
    return nc, list(dbg_out.keys())


_CACHE = {}


def _prep_shared(inp):
    f32 = np.float32
    sh = {}
    for k in ("ada_w", "w_in", "w_oa", "w_ob", "w_o", "ffn_w_in", "ffn_w_out"):
        sh[k] = np.ascontiguousarray(inp[k], dtype=f32)
    sh["ada_bT"] = np.ascontiguousarray(inp["ada_b"].reshape(L, 48, 128).transpose(2, 0, 1))
    sh["n1T"] = np.ascontiguousarray(inp["norm1_w"].reshape(L, 8, 128).transpose(2, 0, 1))
    sh["n2T"] = np.ascontiguousarray(inp["norm2_w"].reshape(L, 8, 128).transpose(2, 0, 1))
    sh["cwT"] = np.ascontiguousarray(inp["ffn_conv_w"].reshape(L, 3, 22, 128).transpose(3, 0, 1, 2))
    sh["cbT"] = np.ascontiguousarray(inp["ffn_conv_b"].reshape(L, 22, 128).transpose(2, 0, 1))
    sh["gnT"] = np.ascontiguousarray(inp["ret_gn_w"].reshape(L, 8, 128).transpose(2, 0, 1))
    sh["nfw"] = np.ascontiguousarray(np.broadcast_to(inp["normf_w"][None], (128, D)))
    pos = inp["cmp_pos"]
    pT = pos.transpose(3, 0, 1, 2)
    sh["posT"] = np.ascontiguousarray(np.concatenate([pT, pT], 0))
    w1 = inp["cmp_w1"].reshape(L, 2, 32, 64, 64)
    bd = np.zeros((128, L, 2, 32, 128), f32)
    w1t = w1.transpose(3, 0, 1, 2, 4)
    bd[0:64, :, :, :, 0:64] = w1t
    bd[64:128, :, :, :, 64:128] = w1t
    sh["w1bd"] = bd
    w2 = inp["cmp_w2"]
    bd2 = np.zeros((128, L, 2, 128), f32)
    w2t = w2.transpose(2, 0, 1, 3)
    bd2[0:64, :, :, 0:64] = w2t
    bd2[64:128, :, :, 64:128] = w2t
    sh["w2bd"] = bd2
    return sh


def kernel(**inp):
    return _run(inp)


def _run(inp, cores=None, stop=None, dbg_names=()):
    consts, sdec, sdecs = _consts()
    shapes = {k: v.shape for k, v in consts.items()}
    key = ("prog", stop, tuple(dbg_names))
    if key not in _CACHE:
        _CACHE[key] = build_program(shapes, sdec, sdecs, stop=stop, dbg_names=dbg_names)
    nc, dbgs = _CACHE[key]
    sh = _prep_shared(inp)
    f32 = np.float32
    in_maps = []
    for c in range(8):
        m = dict(sh)
        for k, v in consts.items():
            m["c_" + k] = v
        m["xp"] = np.ascontiguousarray(inp["x_prompt"][c], dtype=f32)
        m["xs"] = np.ascontiguousarray(inp["x_sample"][4 * c:4 * c + 4].reshape(NS, D), dtype=f32)
        cc = np.concatenate([inp["c_prompt"][c:c + 1], inp["c_sample"][4 * c:4 * c + 4]], 0)
        m["cT"] = np.ascontiguousarray(cc.reshape(5, 8, 128).transpose(2, 1, 0), dtype=f32)
        m["st_ret"] = np.ascontiguousarray(inp["state_ret"][:, 4 * c:4 * c + 4], dtype=f32)
        m["st_conv"] = np.ascontiguousarray(inp["state_conv"][:, 4 * c:4 * c + 4], dtype=f32)
        m["st_win"] = np.ascontiguousarray(inp["state_win_kv"][:, 4 * c:4 * c + 4].reshape(L, 4, 512, 256), dtype=f32)
        for i in range(L):
            m["cache_cmp%d" % i] = inp["cache_cmp_kv"][i].reshape(2560 * 128, 256)
            m["cache_sel%d" % i] = inp["cache_sel_kv"][i].reshape(2560 * 128, 256)
        pt = inp["page_table"][4 * c:4 * c + 4].astype(np.int32)
        m["ptab"] = np.ascontiguousarray(np.broadcast_to(pt[None], (128, 4, 64)))
        in_maps.append(m)
    if cores is not None:
        res = run_bass_kernel_spmd(nc, [in_maps[c] for c in cores], core_ids=list(range(len(cores))))
        return res.results
    res = run_bass_kernel_spmd(nc, in_maps, core_ids=list(range(8)))
    R_ = res.results

    def cat(name, shape_fn):
        return np.stack([shape_fn(R_[c][name]) for c in range(8)])

    y_prompt = np.stack([R_[c]["y_p"] for c in range(8)])
    y_sample = np.concatenate([R_[c]["y_s"].reshape(4, 4, D) for c in range(8)], 0)

    def kvp(name, tlen):
        a = np.stack([R_[c][name] for c in range(8)], 1)
        return a.reshape(L, 8, tlen, 2, 2, 64)

    def kvs(name):
        a = np.concatenate([R_[c][name].reshape(L, 4, 4, 256) for c in range(8)], 1)
        return a.reshape(L, 32, 4, 2, 2, 64)

    ncmp_p = kvp("ncmp_p", T)
    ncmp_s = kvs("ncmp_s")
    nsel_p = kvp("nsel_p", T)
    nsel_s = kvs("nsel_s")
    nwin_p = kvp("nwin_p", 512)
    nwin_s = np.concatenate([R_[c]["nwin_s"] for c in range(8)], 1).reshape(L, 32, 512, 2, 2, 64)
    nret_p = np.stack([R_[c]["nret_p"] for c in range(8)], 1)
    nret_s = np.concatenate([R_[c]["nret_s"] for c in range(8)], 1)
    nconv_p = np.stack([R_[c]["nconv_p"] for c in range(8)], 1)
    nconv_s = np.concatenate([R_[c]["nconv_s"] for c in range(8)], 1)
    outs = (y_prompt, y_sample, ncmp_p, ncmp_s, nsel_p, nsel_s, nwin_p, nwin_s, nret_p, nret_s, nconv_p, nconv_s)
    return tuple(np.ascontiguousarray(o, dtype=np.float32) for o in outs)
```
